# Optimizing a Trainium2 kernel written in Bass

```python
import math
import jax
import jax.numpy as jnp
from jax import lax
import numpy as np

D_MODEL = 1024
BATCH = 4
SEQ = 8192
DEPTH = 4

CTX_LEN = 256
GRID_W = 64
N_MIXERS = 3
N_LAYERS_A = (DEPTH + 2) // 3
N_LAYERS_B = (DEPTH + 1) // 3
N_LAYERS_C = DEPTH // 3
N_MOD = 9
D_FF = 2816
EPS = 1e-6

HG_HEADS = 8
HG_DK = 128
HG_FDIM = HG_HEADS * HG_DK
HG_DV = D_MODEL // HG_HEADS
HG_IN = 3 * HG_FDIM + 2 * D_MODEL
HG_CHUNK = 64

POOL_WINDOWS = (2, 4, 8, 16)
POOL_GROUPS = len(POOL_WINDOWS)
POOL_GC = D_MODEL // POOL_GROUPS

HY_EMB = 33
HY_BANDS = (HY_EMB - 1) // 2
HY_ORDER = 64
HY_SHORT = 3
HY_TARGET = 1e-2
HY_FAST_PCT = 0.3
HY_SLOW_PCT = 1.5

kernel_name = 'hybrid_hgrn2_pool_hyena_dit'


def rms_norm(x):
    xf = x.astype(jnp.float32)
    return (xf * lax.rsqrt(jnp.mean(xf * xf, axis=-1, keepdims=True) + EPS)).astype(x.dtype)


def modulate(h, shift, scale):
    return rms_norm(h) * (1 + scale) + shift


def swiglu(y, w_gate, w_up, w_down):
    return (jax.nn.silu(y @ w_gate) * (y @ w_up)) @ w_down


def _chunk_gla(q, k, v, log_f, s0):
    bsz, nh, L, _ = q.shape
    dv = v.shape[-1]
    n_chunks = L // HG_CHUNK

    def to_chunks(a):
        return jnp.moveaxis(a.reshape(bsz, nh, n_chunks, HG_CHUNK, a.shape[-1]), 2, 0)

    lower = jnp.tril(jnp.ones((HG_CHUNK, HG_CHUNK), dtype=bool))[:, :, None]

    def step(state, inp):
        qc, kc, vc, gc = inp
        b = jnp.cumsum(gc, axis=2)
        diff = b[:, :, :, None, :] - b[:, :, None, :, :]
        decay = jnp.where(lower, jnp.exp(jnp.where(lower, diff, 0.0)), 0.0)
        scores = jnp.einsum('bhtd,bhsd,bhtsd->bhts', qc, kc, decay)
        o = (jnp.einsum('bhts,bhse->bhte', scores, vc)
             + jnp.einsum('bhtd,bhde->bhte', qc * jnp.exp(b), state))
        b_last = b[:, :, -1:, :]
        new_state = (jnp.exp(b_last[:, :, 0, :, None]) * state
                     + jnp.einsum('bhsd,bhse->bhde', kc * jnp.exp(b_last - b), vc))
        return new_state, o

    s_fin, o = lax.scan(step, s0, (to_chunks(q), to_chunks(k), to_chunks(v), to_chunks(log_f)))
    return jnp.moveaxis(o, 0, 2).reshape(bsz, nh, L, dv), s_fin


def hgrn2_mixer(u_lat, u_ctx, lower_bound, w_in, norm_gain, w_out, ctx_out):
    lb = lower_bound.reshape(2, 1, HG_HEADS, 1, HG_DK)

    def features(u):
        bsz, L, _ = u.shape
        q, z_f, z_b, inp, gate = jnp.split(
            u @ w_in, [HG_FDIM, 2 * HG_FDIM, 3 * HG_FDIM, 3 * HG_FDIM + D_MODEL], axis=-1)
        heads = lambda a: a.reshape(bsz, L, HG_HEADS, -1).transpose(0, 2, 1, 3).astype(jnp.float32)
        z = jnp.stack([heads(z_f), heads(z_b)])
        f = lb + (1 - lb) * jax.nn.sigmoid(z)
        log_f = jnp.log(f)
        k = (1 - lb) * jax.nn.sigmoid(-z)
        return heads(jax.nn.silu(q)), k, heads(inp), log_f, gate

    q_c, k_c, v_c, g_c, gate_c = features(u_ctx)
    q_l, k_l, v_l, g_l, gate_l = features(u_lat)
    s0 = jnp.zeros(q_c.shape[:2] + (HG_DK, HG_DV), jnp.float32)
    rev = lambda a: jnp.flip(a, axis=-2)

    o_c_f, s_f = _chunk_gla(q_c, k_c[0], v_c, g_c[0], s0)
    o_c_b, s_b = _chunk_gla(rev(q_c), rev(k_c[1]), rev(v_c), rev(g_c[1]), s0)
    o_l_f, _ = _chunk_gla(q_l, k_l[0], v_l, g_l[0], s_f)
    o_l_b, _ = _chunk_gla(rev(q_l), rev(k_l[1]), rev(v_l), rev(g_l[1]), s_b)

    def readout(o, gate, u):
        bsz, _, L, _ = o.shape
        o = rms_norm(o).transpose(0, 2, 1, 3).reshape(bsz, L, D_MODEL).astype(u.dtype)
        return (o * norm_gain * jax.nn.silu(gate)) @ w_out

    y_lat = readout(o_l_f + rev(o_l_b), gate_l, u_lat)
    y_ctx = readout(o_c_f + rev(o_c_b), gate_c, u_ctx) if ctx_out else None
    return y_lat, y_ctx


def _window_bounds(n, w):
    pos = jnp.arange(n)
    return jnp.clip(pos - w // 2, 0, n), jnp.clip(pos + (w - w // 2), 0, n)


def _box_mean_1d(g, w):
    L = g.shape[1]
    cs = jnp.pad(jnp.cumsum(g, axis=1), ((0, 0), (1, 0), (0, 0)))
    lo, hi = _window_bounds(L, w)
    return (cs[:, hi] - cs[:, lo]) / (hi - lo).astype(jnp.float32)[None, :, None]


def _box_mean_2d(g, w):
    R, W = g.shape[1], g.shape[2]
    sat = jnp.pad(jnp.cumsum(jnp.cumsum(g, axis=1), axis=2), ((0, 0), (1, 0), (1, 0), (0, 0)))
    rlo, rhi = _window_bounds(R, w)
    clo, chi = _window_bounds(W, w)
    s = (sat[:, rhi[:, None], chi[None, :]] - sat[:, rlo[:, None], chi[None, :]]
         - sat[:, rhi[:, None], clo[None, :]] + sat[:, rlo[:, None], clo[None, :]])
    cnt = ((rhi - rlo)[:, None] * (chi - clo)[None, :]).astype(jnp.float32)
    return s / cnt[None, :, :, None]


def pool_mixer(u, on_grid, w_pool, scale):
    bsz, L, _ = u.shape
    uf = u.astype(jnp.float32)
    outs = []
    for gi, w in enumerate(POOL_WINDOWS):
        g = uf[..., gi * POOL_GC:(gi + 1) * POOL_GC]
        if on_grid:
            rows = L // GRID_W
            m = _box_mean_2d(g.reshape(bsz, rows, GRID_W, POOL_GC), w).reshape(bsz, L, POOL_GC)
        else:
            m = _box_mean_1d(g, w)
        outs.append(m - g)
    pooled = jnp.stack(outs, axis=2).astype(u.dtype)
    y = jnp.einsum('blgc,gce->blge', pooled, w_pool).reshape(bsz, L, D_MODEL)
    return y * scale


def _hyena_filter_freq(L, w1, b1, w2, b2, w3, b3, w4, sin_freq):
    pos = jnp.arange(L, dtype=jnp.float32)
    t = jnp.linspace(0.0, 1.0, L, dtype=jnp.float32)
    ang = (2 * math.pi * pos / L)[:, None] * jnp.linspace(1e-4, HY_BANDS - 1, HY_BANDS, dtype=jnp.float32)[None, :]
    feats = jnp.concatenate([t[:, None], jnp.cos(ang), -jnp.sin(ang)], axis=1)
    h = jnp.sin(sin_freq * (feats @ w1 + b1))
    h = jnp.sin(sin_freq * (h @ w2 + b2))
    h = jnp.sin(sin_freq * (h @ w3 + b3))
    h = (h @ w4).astype(jnp.float32).reshape(L, 2, D_MODEL)
    deltas = jnp.abs(jnp.linspace(math.log(HY_TARGET) / HY_SLOW_PCT, math.log(HY_TARGET) / HY_FAST_PCT,
                                  D_MODEL, dtype=jnp.float32))
    h = h * jnp.exp(-t[:, None, None] * deltas)
    k_circ = jnp.concatenate([h[:, 0], jnp.zeros((1, D_MODEL), jnp.float32), h[:0:-1, 1]], axis=0)
    return jnp.fft.rfft(k_circ, axis=0)


def hyena_mixer(u, w_in, b_in, conv_w, conv_b, w1, b1, w2, b2, w3, b3, w4, sin_freq, filt_bias, w_out, b_out):
    L = u.shape[1]
    kf = _hyena_filter_freq(L, w1, b1, w2, b2, w3, b3, w4, sin_freq)
    z = u @ w_in + b_in
    zp = jnp.pad(z, ((0, 0), (1, 1), (0, 0)))
    z = zp[:, :-2] * conv_w[0] + zp[:, 1:-1] * conv_w[1] + zp[:, 2:] * conv_w[2] + conv_b
    x0, x1, v = jnp.split(z, 3, axis=-1)
    v = v * x1
    conv = jnp.fft.irfft(jnp.fft.rfft(v.astype(jnp.float32), n=2 * L, axis=1) * kf[None], n=2 * L, axis=1)[:, :L]
    y = x0 * (conv.astype(u.dtype) + v * filt_bias)
    return y @ w_out + b_out


def setup_inputs(seed: int = 0) -> dict:
    key = jax.random.key(seed)
    ks = iter(jax.random.split(key, 40))

    def nrm(shape, scale):
        return jax.random.normal(next(ks), shape, jnp.float32) * scale

    def ones_noise(shape):
        return 1.0 + nrm(shape, 0.1)

    D = D_MODEL
    return {
        'x': nrm((BATCH, SEQ, D), 1.0),
        'c': nrm((BATCH, D), 1.0),
        'ctx': nrm((BATCH, CTX_LEN, D), 1.0),
        'c_ctx': nrm((D,), 1.0),
        'w_ada': nrm((DEPTH, D, N_MOD * D), 0.5 * D ** -0.5),
        'b_ada': nrm((DEPTH, N_MOD * D), 0.02),
        'ffn_w_gate': nrm((DEPTH, 2, D, D_FF), D ** -0.5),
        'ffn_w_up': nrm((DEPTH, 2, D, D_FF), D ** -0.5),
        'ffn_w_down': nrm((DEPTH, 2, D_FF, D), D_FF ** -0.5),
        'hg_w_in': nrm((N_LAYERS_A, D, HG_IN), D ** -0.5),
        'hg_lb_logits': nrm((2, N_LAYERS_A, HG_FDIM), 0.5),
        'hg_norm_gain': ones_noise((N_LAYERS_A, D)),
        'hg_w_out': nrm((N_LAYERS_A, D, D), D ** -0.5),
        'pool_w': nrm((N_LAYERS_B, POOL_GROUPS, POOL_GC, POOL_GC), POOL_GC ** -0.5),
        'pool_scale': ones_noise((N_LAYERS_B, D)),
        'hy_w_in': nrm((N_LAYERS_C, D, 3 * D), D ** -0.5),
        'hy_b_in': nrm((N_LAYERS_C, 3 * D), 0.02),
        'hy_conv_w': nrm((N_LAYERS_C, HY_SHORT, 3 * D), HY_SHORT ** -0.5),
        'hy_conv_b': nrm((N_LAYERS_C, 3 * D), 0.02),
        'hy_w1': nrm((N_LAYERS_C, HY_EMB, HY_ORDER), HY_EMB ** -0.5),
        'hy_b1': nrm((N_LAYERS_C, HY_ORDER), 0.02),
        'hy_w2': nrm((N_LAYERS_C, HY_ORDER, HY_ORDER), HY_ORDER ** -0.5),
        'hy_b2': nrm((N_LAYERS_C, HY_ORDER), 0.02),
        'hy_w3': nrm((N_LAYERS_C, HY_ORDER, HY_ORDER), HY_ORDER ** -0.5),
        'hy_b3': nrm((N_LAYERS_C, HY_ORDER), 0.02),
        'hy_w4': nrm((N_LAYERS_C, HY_ORDER, 2 * D), 0.02 * HY_ORDER ** -0.5),
        'hy_sin_freq': ones_noise((N_LAYERS_C, HY_ORDER)),
        'hy_filt_bias': nrm((N_LAYERS_C, D), 0.5),
        'hy_w_out': nrm((N_LAYERS_C, D, D), D ** -0.5),
        'hy_b_out': nrm((N_LAYERS_C, D), 0.02),
        'final_gain': ones_noise((D,)),
    }


def reference(x, c, ctx, c_ctx, w_ada, b_ada, ffn_w_gate, ffn_w_up, ffn_w_down,
              hg_w_in, hg_lb_logits, hg_norm_gain, hg_w_out, pool_w, pool_scale,
              hy_w_in, hy_b_in, hy_conv_w, hy_conv_b, hy_w1, hy_b1, hy_w2, hy_b2, hy_w3, hy_b3,
              hy_w4, hy_sin_freq, hy_filt_bias, hy_w_out, hy_b_out, final_gain):
    p = jax.nn.softmax(hg_lb_logits.astype(jnp.float32), axis=1)
    lower_bounds = jnp.cumsum(p, axis=1) - p[:, :1]

    h, hc = x, ctx
    for i in range(DEPTH):
        kind, j = i % N_MIXERS, i // N_MIXERS
        last = i == DEPTH - 1
        ctx_live = (not last) or kind == 0
        mod = (jax.nn.silu(c) @ w_ada[i] + b_ada[i]).reshape(-1, N_MOD, 1, D_MODEL)
        h = h + 0.5 * mod[:, 2] * swiglu(modulate(h, mod[:, 0], mod[:, 1]),
                                         ffn_w_gate[i, 0], ffn_w_up[i, 0], ffn_w_down[i, 0])
        u = modulate(h, mod[:, 3], mod[:, 4])
        uc = None
        if ctx_live:
            mod_c = (jax.nn.silu(c_ctx) @ w_ada[i] + b_ada[i]).reshape(N_MOD, D_MODEL)
            hc = hc + 0.5 * mod_c[2] * swiglu(modulate(hc, mod_c[0], mod_c[1]),
                                              ffn_w_gate[i, 0], ffn_w_up[i, 0], ffn_w_down[i, 0])
            uc = modulate(hc, mod_c[3], mod_c[4])

        if kind == 0:
            y, yc = hgrn2_mixer(u, uc, lower_bounds[:, j], hg_w_in[j], hg_norm_gain[j], hg_w_out[j], not last)
        elif kind == 1:
            y = pool_mixer(u, True, pool_w[j], pool_scale[j])
            yc = None if last else pool_mixer(uc, False, pool_w[j], pool_scale[j])
        else:
            hy_args = (hy_w_in[j], hy_b_in[j], hy_conv_w[j], hy_conv_b[j], hy_w1[j], hy_b1[j], hy_w2[j],
                       hy_b2[j], hy_w3[j], hy_b3[j], hy_w4[j], hy_sin_freq[j], hy_filt_bias[j],
                       hy_w_out[j], hy_b_out[j])
            y = hyena_mixer(u, *hy_args)
            yc = None if last else hyena_mixer(uc, *hy_args)

        h = h + mod[:, 5] * y
        h = h + 0.5 * mod[:, 8] * swiglu(modulate(h, mod[:, 6], mod[:, 7]),
                                         ffn_w_gate[i, 1], ffn_w_up[i, 1], ffn_w_down[i, 1])
        if not last:
            hc = hc + mod_c[5] * yc
            hc = hc + 0.5 * mod_c[8] * swiglu(modulate(hc, mod_c[6], mod_c[7]),
                                              ffn_w_gate[i, 1], ffn_w_up[i, 1], ffn_w_down[i, 1])

    return rms_norm(h) * final_gain
```

```python
import contextlib
import math
import numpy as np
import concourse.bass as bass
import concourse.mybir as mybir
from concourse.bass_utils import run_bass_kernel_spmd

F32 = mybir.dt.float32
BF16 = mybir.dt.bfloat16
ALU = mybir.AluOpType
AF = mybir.ActivationFunctionType
AX = mybir.AxisListType

D = 1024
L = 8192
LC = 256
NT = L + LC
DFF = 2816
DEPTH = 4
EPS = 1e-6
NCORES = 4
PI = float(np.pi)


class Buf:
    def __init__(self, t, name):
        self.t = t
        self.name = name
        self.last_w = None
        self.readers = []
        self.dsem = {}
        self.dcount = {}

    def __getitem__(self, k):
        return self.t[k]


class Eng:
    def __init__(self, fw, eng, name):
        self.eng = eng
        self.name = name
        self.sem = fw.new_sem("c_" + name)
        self.count = 0
        self.known = {}


class FW:
    def __init__(self, nc):
        self.nc = nc
        self.es = contextlib.ExitStack()
        self.nsem = 0
        self.pe = Eng(self, nc.tensor, "pe")
        self.act = Eng(self, nc.scalar, "act")
        self.dve = Eng(self, nc.vector, "dve")
        self.pool = Eng(self, nc.gpsimd, "pool")
        self.sp = Eng(self, nc.sync, "sp")
        self.engs = [self.pe, self.act, self.dve, self.pool, self.sp]
        self.bufs = []
        self.ack = self.new_sem("ack")
        self.clr = self.new_sem("clr")
        self.bar_n = 0
        self.dma_events = []
        self.dsem_pool = []
        self.nbuf = 0

    def new_sem(self, name):
        s = self.es.enter_context(self.nc.semaphore(name + "_%d" % self.nsem))
        self.nsem += 1
        return s

    def sb(self, shape, dt=F32, name="sb", stack=None):
        self.nbuf += 1
        t = (stack or self.es).enter_context(self.nc.sbuf_tensor("%s_%d" % (name, self.nbuf), list(shape), dt))
        b = Buf(t, name)
        self.bufs.append(b)
        return b

    def ps(self, shape, dt=F32, name="ps", stack=None):
        self.nbuf += 1
        t = (stack or self.es).enter_context(self.nc.psum_tensor("%s_%d" % (name, self.nbuf), list(shape), dt))
        b = Buf(t, name)
        self.bufs.append(b)
        return b

    def release(self, bufs):
        for b in bufs:
            for k, s_ in b.dsem.items():
                self.dsem_pool.append(s_)
            b.dsem = {}
            b.dcount = {}
            if b in self.bufs:
                self.bufs.remove(b)

    def _wait(self, e, ev):
        if ev is None:
            return
        sem, val, src = ev
        if src is e and e is self.pe:
            return
        key = id(sem)
        if e.known.get(key, 0) >= val:
            return
        e.eng.wait_ge(sem, val)
        e.known[key] = val

    def _deps(self, e, reads, writes):
        for b in reads:
            self._wait(e, b.last_w)
        for b in writes:
            self._wait(e, b.last_w)
            for r in b.readers:
                self._wait(e, r)

    def op(self, e, fn, reads=(), writes=()):
        self._deps(e, reads, writes)
        ins = fn()
        e.count += 1
        ins.then_inc(e.sem, 1)
        ev = (e.sem, e.count, e)
        for b in reads:
            b.readers.append(ev)
            if len(b.readers) > 16:
                latest = {}
                for r in b.readers:
                    k = id(r[0])
                    if k not in latest or latest[k][1] < r[1]:
                        latest[k] = r
                b.readers = list(latest.values())
        for b in writes:
            b.last_w = ev
            b.readers = []
        return ins

    def dma(self, e, out, in_, buf, is_load, **kw):
        if e.name not in buf.dsem:
            buf.dsem[e.name] = self.dsem_pool.pop() if self.dsem_pool else self.new_sem("d")
            buf.dcount[e.name] = 0
        if is_load:
            self._deps(e, [], [buf])
        else:
            self._deps(e, [buf], [])
        ins = e.eng.dma_start(out=out, in_=in_, **kw)
        buf.dcount[e.name] += 1
        ins.then_inc(buf.dsem[e.name], 16)
        ev = (buf.dsem[e.name], 16 * buf.dcount[e.name], None)
        if is_load:
            buf.last_w = ev
            buf.readers = []
        else:
            buf.readers.append(ev)
        self.dma_events.append(ev)
        return ins

    def barrier(self):
        evs = [(x.sem, x.count, x) for x in self.engs if x.count > 0]
        dm = {}
        for (s, v, _) in self.dma_events:
            if id(s) not in dm or dm[id(s)][1] < v:
                dm[id(s)] = (s, v, None)
        for e in self.engs:
            for ev in evs:
                if ev[2] is e:
                    continue
                self._wait(e, ev)
            for ev in dm.values():
                self._wait(e, ev)
        self.dma_events = []
        for b in self.bufs:
            b.last_w = None
            b.readers = []
        n = len(self.engs)
        self.bar_n += 1
        for e in self.engs:
            e.eng.sem_inc(self.ack, 1)
        for e in self.engs:
            e.eng.wait_ge(self.ack, n * self.bar_n)
            e.eng.sem_clear(e.sem)
            e.count = 0
            if e is self.sp:
                for b in self.bufs:
                    for k in b.dsem:
                        if b.dcount[k] > 0:
                            e.eng.sem_clear(b.dsem[k])
                            b.dcount[k] = 0
            e.eng.sem_inc(self.clr, 1)
        for e in self.engs:
            e.eng.wait_ge(self.clr, n * self.bar_n)
            e.known = {}

    def mm(self, out_b, out_ap, lhsT_b, lhsT_ap, rhs_b, rhs_ap, start, stop, **kw):
        nc = self.nc
        return self.op(self.pe, lambda: nc.tensor.matmul(out_ap, lhsT_ap, rhs_ap, start=start, stop=stop, **kw),
                       reads=[lhsT_b, rhs_b], writes=[out_b])


VEC_SLOTS = {}


def _vec_layout():
    off = 0
    lay = {}

    def add(name, n):
        nonlocal off
        lay[name] = (off, n // 128)
        off += n // 128

    add("final_gain", D)
    for j in range(2):
        add("hg_gain%d" % j, D)
    add("pool_scale", D)
    add("hy_b_in", 3 * D)
    for t in range(3):
        add("hy_conv_w%d" % t, 3 * D)
    add("hy_conv_b", 3 * D)
    add("hy_filt_bias", D)
    add("hy_b_out", D)
    add("hy_negdelta", D)
    return lay, off


VEC_LAY, NV = _vec_layout()


class Prog:
    def __init__(self, dbg=None, stop_after=None):
        self.dbg = dbg or []
        self.stop_after = stop_after
        nc = bass.Bass("TRN2", target_bir_lowering=False)
        self.nc = nc
        self.fw = FW(nc)
        self.inp = {}
        self.out = None

    def din(self, name, shape, dt=F32):
        t = self.nc.dram_tensor(name, list(shape), dt, kind="ExternalInput").ap()
        self.inp[name] = t
        return t

    def scratch(self, name, shape, dt=F32):
        if name in self.dbg:
            return self.nc.dram_tensor(name, list(shape), dt, kind="ExternalOutput").ap()
        return self.nc.dram_tensor(name, list(shape), dt).ap()

    def build(self):
        nc, fw = self.nc, self.fw
        x = self.din("x", [L, D])
        ctx = self.din("ctx", [LC, D])
        csT = self.din("csT", [128, 8, 2])
        w_ada = self.din("w_ada", [DEPTH, D, 9 * D])
        b_ada = self.din("b_ada", [DEPTH, 9 * D])
        self.wg = self.din("ffn_w_gate", [DEPTH, 2, D, DFF])
        self.wu = self.din("ffn_w_up", [DEPTH, 2, D, DFF])
        self.wd = self.din("ffn_w_down", [DEPTH, 2, DFF, D])
        vecs = self.din("vecs", [128, NV])
        ident = self.din("ident", [128, 128])
        self.pool_w = self.din("pool_w", [4, 256, 256])
        self.pool_inv = self.din("pool_inv", [4, NT])
        self.declare_mixer_inputs()
        self.outT = self.nc.dram_tensor("out", [L, D], F32, kind="ExternalOutput").ap()

        self.HT = self.scratch("HT", [D, NT])
        self.UT = self.scratch("UT", [D, NT])
        self.YT = self.scratch("YT", [D, NT], BF16)
        self.MT = self.scratch("MT", [D, NT])
        self.MTv = self.MT.rearrange("(c p) t -> p c t", p=128)
        self.PLT = self.scratch("PLT", [D, NT], BF16)
        self.PLTv = self.PLT.rearrange("(c p) t -> p c t", p=128)
        self.HTv = self.HT.rearrange("(c p) t -> p c t", p=128)
        self.UTv = self.UT.rearrange("(c p) t -> p c t", p=128)
        self.YTv = self.YT.rearrange("(c p) t -> p c t", p=128)

        self.ident = fw.sb([128, 128], F32, "ident")
        self.vecs = fw.sb([128, NV], F32, "vecs")
        self.mod = fw.sb([128, DEPTH, 72, 2], F32, "mod")
        self.ones_bf = fw.sb([128, 128], BF16, "ones_bf")
        self.eps_t = fw.sb([128, 1], F32, "eps")
        fw.dma(fw.sp, self.ident[:], ident, self.ident, True)
        fw.dma(fw.sp, self.vecs[:], vecs, self.vecs, True)
        fw.op(fw.dve, lambda: nc.vector.memset(self.ones_bf[:], 1.0 / D), writes=[self.ones_bf])
        fw.op(fw.dve, lambda: nc.vector.memset(self.eps_t[:], EPS), writes=[self.eps_t])

        self.phase_mods(csT, w_ada, b_ada)
        if self.stop_after == "mods":
            return self.finish()
        self.phase_in(x, ctx)
        if self.stop_after == "in":
            return self.finish()
        for li in range(DEPTH):
            kind = li % 3
            last = li == DEPTH - 1
            self.phase_ffn(li, 0, 0, make_u=False, ctx_live=True)
            if self.stop_after == "ffnA_%d" % li:
                return self.finish()
            self.phase_ffn(li, 0, 1, make_u=True, ctx_live=True)
            if self.stop_after == "ffn1_%d" % li:
                return self.finish()
            if kind == 1:
                self.phase_pool(li)
            elif kind == 0:
                self.phase_hgrn(li)
            else:
                self.phase_hyena(li)
            if self.stop_after == "mix_%d" % li:
                return self.finish()
            self.phase_ffn(li, 1, 0, make_u=False, ctx_live=not last)
            self.phase_ffn(li, 1, 1, make_u=False, ctx_live=not last)
            if self.stop_after == "ffn2_%d" % li:
                return self.finish()
        self.phase_out()
        return self.finish()

    def build_mixer_test(self, li):
        nc, fw = self.nc, self.fw
        vecs = self.din("vecs", [128, NV])
        ident = self.din("ident", [128, 128])
        self.pool_w = self.din("pool_w", [4, 256, 256])
        self.pool_inv = self.din("pool_inv", [4, NT])
        self.UT = self.din("UT", [D, NT])
        self.MT = self.nc.dram_tensor("MT", [D, NT], F32, kind="ExternalOutput").ap()
        self.UTv = self.UT.rearrange("(c p) t -> p c t", p=128)
        self.MTv = self.MT.rearrange("(c p) t -> p c t", p=128)
        self.PLT = self.scratch("PLT", [D, NT], BF16)
        self.PLTv = self.PLT.rearrange("(c p) t -> p c t", p=128)
        self.ident = fw.sb([128, 128], F32, "ident")
        self.vecs = fw.sb([128, NV], F32, "vecs")
        fw.dma(fw.sp, self.ident[:], ident, self.ident, True)
        fw.dma(fw.sp, self.vecs[:], vecs, self.vecs, True)
        self.eps_t = fw.sb([128, 1], F32, "eps")
        fw.op(fw.dve, lambda: nc.vector.memset(self.eps_t[:], EPS), writes=[self.eps_t])
        self.mixer_inputs()
        kind = li % 3
        if kind == 1:
            self.phase_pool(li)
        elif kind == 0:
            self.phase_hgrn(li)
        else:
            self.phase_hyena(li)
        return self.finish()

    def mixer_inputs(self):
        pass

    def finish(self):
        if "MODD" in self.dbg:
            md = self.nc.dram_tensor("MODD", [128, DEPTH * 72 * 2], F32, kind="ExternalOutput").ap()
            self.fw.dma(self.fw.sp, md, self.mod[:].rearrange("p l j m -> p (l j m)"), self.mod, False)
        self.fw.barrier()
        self.fw.es.close()
        return self.nc

    def vec(self, name, c):
        off, n = VEC_LAY[name]
        return self.vecs[:, off + c:off + c + 1]

    def modv(self, li, k, c, m):
        return self.mod[:, li, k * 8 + c, m:m + 1]

    def phase_mods(self, csT, w_ada, b_ada):
        nc, fw = self.nc, self.fw
        st = contextlib.ExitStack()
        loc = []

        def sb(*a, **k):
            b = fw.sb(*a, stack=st, **k)
            loc.append(b)
            return b

        cs = sb([128, 8, 2], F32, "cs")
        ones2 = sb([1, 2], F32, "ones2")
        brow = sb([1, 9 * D], F32, "brow")
        wbuf = [sb([128, 8, 512], F32, "wada") for _ in range(2)]
        pm = fw.ps([128, 144], F32, "pm", stack=st)
        loc.append(pm)
        fw.dma(fw.sp, cs[:], csT, cs, True)
        fw.op(fw.act, lambda: nc.scalar.activation(cs[:], cs[:], AF.Silu), reads=[cs], writes=[cs])
        fw.op(fw.dve, lambda: nc.vector.memset(ones2[:], 1.0), writes=[ones2])
        n = 0
        for li in range(DEPTH):
            fw.dma(fw.sp, brow[:], b_ada[li:li + 1, :], brow, True)
            wv = w_ada[li].rearrange("(kc p) f -> p kc f", p=128)
            for piece in range(18):
                wb = wbuf[n % 2]
                n += 1
                fw.dma(fw.sp, wb[:], wv[:, :, piece * 512:(piece + 1) * 512], wb, True)
                for jj in range(4):
                    j = piece * 4 + jj
                    for kc in range(8):
                        fw.mm(pm, pm[:, 2 * j:2 * j + 2], wb, wb[:, kc, jj * 128:(jj + 1) * 128], cs, cs[:, kc, :],
                              kc == 0, False)
                    fw.mm(pm, pm[:, 2 * j:2 * j + 2], brow, brow[0:1, j * 128:(j + 1) * 128], ones2, ones2[0:1, :],
                          False, True)
            fw.op(fw.dve, lambda: nc.vector.tensor_copy(self.mod[:, li, :, :], pm[:].rearrange("p (j m) -> p j m", m=2)),
                  reads=[pm], writes=[self.mod])
            for k in (1, 4, 7):
                sl = self.mod[:, li, k * 8:(k + 1) * 8, :]
                fw.op(fw.dve, lambda: nc.vector.tensor_scalar(sl, sl, 1.0, None, ALU.add), reads=[self.mod], writes=[self.mod])
            for k in (2, 8):
                sl = self.mod[:, li, k * 8:(k + 1) * 8, :]
                fw.op(fw.dve, lambda: nc.vector.tensor_scalar(sl, sl, 0.5, None, ALU.mult), reads=[self.mod], writes=[self.mod])
        fw.barrier()
        fw.release(loc)
        st.close()

    def phase_in(self, x, ctx):
        nc, fw = self.nc, self.fw
        st = contextlib.ExitStack()
        loc = []

        def sb(*a, **k):
            b = fw.sb(*a, stack=st, **k)
            loc.append(b)
            return b

        xin = [sb([128, 4, D], F32, "xin") for _ in range(2)]
        xo = [sb([128, 8, 512], F32, "xo") for _ in range(2)]
        pt = [fw.ps([128, 8, 128], F32, "ptr", stack=st) for _ in range(2)]
        loc.extend(pt)
        npt = 0
        def issue_load(ti):
            ng = 4 if ti < 16 else 2
            xi = xin[ti % 2]
            if ti < 16:
                src = x[ti * 512:(ti + 1) * 512, :].rearrange("(g p) d -> p g d", p=128)
            else:
                src = ctx.rearrange("(g p) d -> p g d", p=128)
            fw.dma(fw.sp, xi[:, 0:ng, :], src, xi, True)

        issue_load(0)
        for ti in range(17):
            ng = 4 if ti < 16 else 2
            xi = xin[ti % 2]
            xout = xo[ti % 2]
            if ti + 1 < 17:
                issue_load(ti + 1)
            for g in range(ng):
                p = pt[npt % 2]
                npt += 1
                for c in range(8):
                    fw.op(fw.pe, lambda: nc.tensor.transpose(p[:, c, :], xi[:, g, c * 128:(c + 1) * 128], self.ident[:]),
                          reads=[xi, self.ident], writes=[p])
                eng = fw.dve if g % 2 == 0 else fw.act
                if eng is fw.dve:
                    fw.op(eng, lambda: nc.vector.tensor_copy(xout[:, :, g * 128:(g + 1) * 128], p[:]), reads=[p], writes=[xout])
                else:
                    fw.op(eng, lambda: nc.scalar.copy(xout[:, :, g * 128:(g + 1) * 128], p[:]), reads=[p], writes=[xout])
            w = ng * 128
            fw.dma(fw.sp, self.HTv[:, :, ti * 512:ti * 512 + w], xout[:, :, 0:w], xout, False)
        fw.barrier()
        fw.release(loc)
        st.close()

    def rms_stats(self, hb, w, sq, ps_stat, rstd):
        nc, fw = self.nc, self.fw
        fw.op(fw.act, lambda: nc.scalar.activation(sq[:, :, 0:w], hb[:, :, 0:w], AF.Square), reads=[hb], writes=[sq])
        for c in range(8):
            fw.mm(ps_stat, ps_stat[:, 0:w], self.ones_bf, self.ones_bf[:], sq, sq[:, c, 0:w], c == 0, c == 7)
        fw.op(fw.act, lambda: nc.scalar.activation(rstd[:, 0:w], ps_stat[:, 0:w], AF.Sqrt, bias=self.eps_t[:, 0:1], scale=1.0),
              reads=[ps_stat, self.eps_t], writes=[rstd])
        fw.op(fw.dve, lambda: nc.vector.reciprocal(rstd[:, 0:w], rstd[:, 0:w]), reads=[rstd], writes=[rstd])

    def modulate(self, hb, w, rstd, tmp, outb, li, k_shift, k_scale, m):
        nc, fw = self.nc, self.fw
        for c in range(8):
            t = tmp[c % 2]
            fw.op(fw.dve, lambda: nc.vector.tensor_tensor(t[:, 0:w], hb[:, c, 0:w], rstd[:, 0:w], ALU.mult),
                  reads=[hb, rstd], writes=[t])
            fw.op(fw.act, lambda: nc.scalar.activation(outb[:, c, 0:w], t[:, 0:w], AF.Identity,
                                                       bias=self.modv(li, k_shift, c, m), scale=self.modv(li, k_scale, c, m)),
                  reads=[t, self.mod], writes=[outb])

    def load_cast(self, dst, dst_ap_fn, src_ap_fn, n, stg, shape_w):
        nc, fw = self.nc, self.fw
        for i in range(n):
            s = stg[i % len(stg)]
            fw.dma(fw.sp, s[:, 0:shape_w], src_ap_fn(i), s, True)
            if i % 2 == 0:
                fw.op(fw.dve, lambda: nc.vector.tensor_copy(dst_ap_fn(i), s[:, 0:shape_w]), reads=[s], writes=[dst])
            else:
                fw.op(fw.act, lambda: nc.scalar.copy(dst_ap_fn(i), s[:, 0:shape_w]), reads=[s], writes=[dst])

    def phase_ffn(self, li, which, half, make_u, ctx_live):
        nc, fw = self.nc, self.fw
        st = contextlib.ExitStack()
        loc = []

        def sb(*a, **k):
            b = fw.sb(*a, stack=st, **k)
            loc.append(b)
            return b

        def ps(*a, **k):
            b = fw.ps(*a, stack=st, **k)
            loc.append(b)
            return b

        FH = DFF // 2
        NFC = FH // 128
        f0 = half * FH
        ks = 0 if which == 0 else 6
        wg_b = sb([128, 8, FH], BF16, "wg")
        wu_b = sb([128, 8, FH], BF16, "wu")
        wd_b = sb([128, NFC, D], BF16, "wd")
        stg = [sb([128, FH], F32, "stg") for _ in range(2)]
        wgv = self.wg[li, which].rearrange("(c p) f -> p c f", p=128)
        wuv = self.wu[li, which].rearrange("(c p) f -> p c f", p=128)
        wdv = self.wd[li, which].rearrange("(c p) d -> p c d", p=128)
        self.load_cast(wg_b, lambda i: wg_b[:, i, :], lambda i: wgv[:, i, f0:f0 + FH], 8, stg, FH)
        self.load_cast(wu_b, lambda i: wu_b[:, i, :], lambda i: wuv[:, i, f0:f0 + FH], 8, stg, FH)
        self.load_cast(wd_b, lambda i: wd_b[:, i, :], lambda i: wdv[:, half * NFC + i, :], NFC, stg, D)

        hb2 = [sb([128, 8, 512], F32, "h") for _ in range(2)]
        yb2 = [sb([128, 8, 512], BF16, "y") for _ in range(2 if half == 1 else 1)]
        sq = sb([128, 8, 512], BF16, "sq")
        ab = sb([128, NFC, 512], BF16, "a")
        rstd = sb([128, 512], F32, "rstd")
        tmp = [sb([128, 512], F32, "tmp") for _ in range(2)]
        sg = [sb([128, 512], F32, "sg") for _ in range(2)]
        add_mix = (which == 1 and half == 0)
        ub = sb([128, 8, 512], F32, "u") if (make_u or add_mix) else None
        ps_stat = ps([128, 512], F32, "pstat")
        ps_g = [ps([128, 512], F32, "pg") for _ in range(2)]
        ps_u = [ps([128, 512], F32, "pu") for _ in range(2)]
        ps_d = [ps([128, 512], F32, "pd") for _ in range(2)]

        ntiles = 17 if ctx_live else 16
        import os
        if os.environ.get('DBG_NOU'):
            make_u = False
        if os.environ.get('DBG_NT'):
            ntiles = int(os.environ['DBG_NT'])
        def issue_load(ti):
            w = 512 if ti < 16 else 256
            c0 = ti * 512
            hb = hb2[ti % 2]
            fw.dma(fw.sp, hb[:, :, 0:w], self.HTv[:, :, c0:c0 + w], hb, True)
            if half == 1:
                yb = yb2[ti % 2]
                fw.dma(fw.sp, yb[:, :, 0:w], self.YTv[:, :, c0:c0 + w], yb, True)

        issue_load(0)
        for ti in range(ntiles):
            w = 512 if ti < 16 else 256
            m = 0 if ti < 16 else 1
            c0 = ti * 512
            hb = hb2[ti % 2]
            yb = yb2[ti % len(yb2)]
            if add_mix:
                fw.dma(fw.sp, ub[:, :, 0:w], self.MTv[:, :, c0:c0 + w], ub, True)
            if ti + 1 < ntiles:
                issue_load(ti + 1)
            if add_mix:
                for c in range(8):
                    fw.op(fw.dve, lambda: nc.vector.scalar_tensor_tensor(hb[:, c, 0:w], ub[:, c, 0:w], self.modv(li, 5, c, m),
                                                                         hb[:, c, 0:w], ALU.mult, ALU.add),
                          reads=[ub, hb, self.mod], writes=[hb])
            if half == 0:
                self.rms_stats(hb, w, sq, ps_stat, rstd)
                self.modulate(hb, w, rstd, tmp, yb, li, ks, ks + 1, m)
                fw.dma(fw.sp, self.YTv[:, :, c0:c0 + w], yb[:, :, 0:w], yb, False)
            for fc in range(NFC):
                pg = ps_g[fc % 2]
                pu = ps_u[fc % 2]
                for c in range(8):
                    fw.mm(pg, pg[:, 0:w], wg_b, wg_b[:, c, fc * 128:(fc + 1) * 128], yb, yb[:, c, 0:w], c == 0, c == 7)
                for c in range(8):
                    fw.mm(pu, pu[:, 0:w], wu_b, wu_b[:, c, fc * 128:(fc + 1) * 128], yb, yb[:, c, 0:w], c == 0, c == 7)
                s = sg[fc % 2]
                fw.op(fw.act, lambda: nc.scalar.activation(s[:, 0:w], pg[:, 0:w], AF.Silu), reads=[pg], writes=[s])
                fw.op(fw.dve, lambda: nc.vector.tensor_tensor(ab[:, fc, 0:w], s[:, 0:w], pu[:, 0:w], ALU.mult),
                      reads=[s, pu], writes=[ab])
            for dc in range(8):
                pd = ps_d[dc % 2]
                for fc in range(NFC):
                    fw.mm(pd, pd[:, 0:w], wd_b, wd_b[:, fc, dc * 128:(dc + 1) * 128], ab, ab[:, fc, 0:w],
                          fc == 0, fc == NFC - 1)
                fw.op(fw.dve, lambda: nc.vector.scalar_tensor_tensor(hb[:, dc, 0:w], pd[:, 0:w], self.modv(li, ks + 2, dc, m),
                                                                     hb[:, dc, 0:w], ALU.mult, ALU.add),
                      reads=[pd, hb, self.mod], writes=[hb])
            fw.dma(fw.sp, self.HTv[:, :, c0:c0 + w], hb[:, :, 0:w], hb, False)
            if make_u:
                self.rms_stats(hb, w, sq, ps_stat, rstd)
                self.modulate(hb, w, rstd, tmp, ub, li, 3, 4, m)
                fw.dma(fw.sp, self.UTv[:, :, c0:c0 + w], ub[:, :, 0:w], ub, False)
        fw.barrier()
        fw.release(loc)
        st.close()


    def declare_mixer_inputs(self):
        self.hg_w_in = self.din("hg_w_in", [2, D, 5 * D])
        self.hg_w_out = self.din("hg_w_out", [2, D, D])
        self.hg_lb = self.din("hg_lb_logits", [2, 2, D])
        self.hg_gain = self.din("hg_norm_gain", [2, D])
        self.tri = self.din("tri", [4, 128, 128])
        self.OT = self.scratch("OT", [NT, D])
        self.declare_hyena_inputs()

    def mixer_inputs(self):
        self.declare_mixer_inputs()

    def proj_tm(self, X, ubf, wb, col0, ncols=1024):
        fw = self.fw
        for n in range(ncols // 512):
            for c in range(8):
                fw.mm(X, X[:, n * 512:(n + 1) * 512], ubf, ubf[:, c, :], wb, wb[:, c, col0 + n * 512:col0 + (n + 1) * 512],
                      c == 0, c == 7)

    def load_w_bf(self, dst, src_view, col_map, stg):
        nc, fw = self.nc, self.fw
        i = 0
        for j, sc in enumerate(col_map):
            for c in range(8):
                s_ = stg[i % 2]
                fw.dma(fw.sp, s_[:], src_view[:, c, sc * 1024:(sc + 1) * 1024], s_, True)
                if i % 2 == 0:
                    fw.op(fw.dve, lambda: nc.vector.tensor_copy(dst[:, c, j * 1024:(j + 1) * 1024], s_[:]), reads=[s_], writes=[dst])
                else:
                    fw.op(fw.act, lambda: nc.scalar.copy(dst[:, c, j * 1024:(j + 1) * 1024], s_[:]), reads=[s_], writes=[dst])
                i += 1

    def phase_hgrn(self, li):
        j = li // 3
        ctx_out = li != DEPTH - 1
        self.hgrn_sweep(li, j, 0)
        self.hgrn_sweep(li, j, 1)
        self.hgrn_readout(li, j, ctx_out)

    def hgrn_sweep(self, li, j, dr):
        nc, fw = self.nc, self.fw
        st = contextlib.ExitStack()
        loc = []

        def sb(*a, **k):
            b = fw.sb(*a, stack=st, **k)
            loc.append(b)
            return b

        wb = sb([128, 8, 3072], BF16, "hgw")
        stg = [sb([128, 1024], F32, "hgstg") for _ in range(2)]
        wv = self.hg_w_in[j].rearrange("(c p) f -> p c f", p=128)
        self.load_w_bf(wb, wv, [0, 1 + dr, 3], stg)
        tri = sb([128, 128], F32, "tri")
        bones = sb([128, 128], F32, "bones")
        ind = sb([128, 128], F32, "ind")
        M8 = sb([128, 8, 128], F32, "M8")
        fw.dma(fw.sp, tri[:], self.tri[dr], tri, True)
        fw.dma(fw.sp, bones[:], self.tri[2], bones, True)
        fw.dma(fw.sp, ind[:], self.tri[3], ind, True)
        for h in range(8):
            fw.op(fw.dve, lambda: nc.vector.tensor_copy(M8[:, h, :], tri[:]), reads=[tri], writes=[M8])
        lbb = sb([128, 1024], F32, "lbb")
        oml = sb([128, 1024], F32, "oml")
        if j == 0:
            fw.op(fw.dve, lambda: nc.vector.memset(lbb[:], 0.0), writes=[lbb])
            fw.op(fw.dve, lambda: nc.vector.memset(oml[:], 1.0), writes=[oml])
        else:
            fw.dma(fw.sp, lbb[:], self.hg_lb[dr, 1:2, :].to_broadcast([128, D]), lbb, True)
            fw.dma(fw.sp, oml[:], self.hg_lb[dr, 0:1, :].to_broadcast([128, D]), oml, True)
            fw.op(fw.dve, lambda: nc.vector.tensor_tensor(lbb[:], lbb[:], oml[:], ALU.subtract), reads=[lbb, oml], writes=[lbb])
            fw.op(fw.act, lambda: nc.scalar.activation(lbb[:], lbb[:], AF.Sigmoid), reads=[lbb], writes=[lbb])
            fw.op(fw.dve, lambda: nc.vector.tensor_scalar(oml[:], lbb[:], -1.0, 1.0, ALU.mult, ALU.add), reads=[lbb], writes=[oml])

        uf = [sb([128, 8, 128], F32, "uf") for _ in range(2)]
        ubf = sb([128, 8, 128], BF16, "ubf")
        qs = sb([128, 1024], F32, "qs")
        fk = sb([128, 1024], F32, "fk")
        lf = sb([128, 1024], F32, "lf")
        vb = sb([128, 1024], BF16, "vb")
        bc = sb([128, 1024], F32, "bc")
        e1 = sb([128, 1024], F32, "e1")
        kt = sb([128, 1024], F32, "kt")
        Kc = [sb([128, 1024], BF16, "Kc") for _ in range(4)]
        Qc = [sb([128, 8, 128], BF16, "Qc") for _ in range(4)]
        qTb = sb([128, 8, 128], BF16, "qTb")
        kTb = sb([128, 8, 128], BF16, "kTb")
        PT = sb([128, 8, 128], BF16, "PT")
        S = sb([128, 1024], F32, "S")
        Sb = sb([128, 1024], BF16, "Sb")
        ebl = sb([128, 8, 4], F32, "ebl")
        ob = [sb([128, 1024], F32, "ob") for _ in range(2)]
        of = [sb([128, 1024], F32, "of") for _ in range(2)] if dr == 1 else None
        X = [fw.ps([128, 1024], F32, "X", stack=st) for _ in range(4)]
        loc.extend(X)
        for q in Qc:
            fw.op(fw.dve, lambda: nc.vector.memset(q[:], 0.0), writes=[q])
        fw.op(fw.dve, lambda: nc.vector.memset(S[:], 0.0), writes=[S])
        fw.op(fw.dve, lambda: nc.vector.memset(Sb[:], 0.0), writes=[Sb])

        if dr == 0:
            groups = [L + 0, L + 128] + [g * 128 for g in range(64)]
            corder = [0, 1, 2, 3]
        else:
            groups = [L + 128, L + 0] + [g * 128 for g in range(63, -1, -1)]
            corder = [3, 2, 1, 0]
        import os
        if os.environ.get("DBG_NG"):
            ng_ = int(os.environ["DBG_NG"])
            groups = [g_ for g_ in groups if g_ >= L or g_ < 128 * ng_]

        def issue_load(gi):
            t0 = groups[gi]
            fw.dma(fw.sp, uf[gi % 2][:], self.UTv[:, :, t0:t0 + 128], uf[gi % 2], True)
            if dr == 1:
                fw.dma(fw.sp, of[gi % 2][:], self.OT[t0:t0 + 128, :], of[gi % 2], True)

        issue_load(0)
        for gi, t0 in enumerate(groups):
            if gi + 1 < len(groups):
                issue_load(gi + 1)
            u = uf[gi % 2]
            fw.op(fw.dve, lambda: nc.vector.tensor_copy(ubf[:], u[:]), reads=[u], writes=[ubf])
            self.proj_tm(X[0], ubf, wb, 0)
            fw.op(fw.act, lambda: nc.scalar.activation(qs[:], X[0][:], AF.Silu), reads=[X[0]], writes=[qs])
            self.proj_tm(X[1], ubf, wb, 1024)
            fw.op(fw.act, lambda: nc.scalar.activation(fk[:], X[1][:], AF.Sigmoid), reads=[X[1]], writes=[fk])
            self.proj_tm(X[2], ubf, wb, 2048)
            fw.op(fw.act, lambda: nc.scalar.copy(vb[:], X[2][:]), reads=[X[2]], writes=[vb])
            fw.op(fw.dve, lambda: nc.vector.tensor_tensor(fk[:], fk[:], oml[:], ALU.mult), reads=[fk, oml], writes=[fk])
            fw.op(fw.dve, lambda: nc.vector.tensor_tensor(fk[:], fk[:], lbb[:], ALU.add), reads=[fk, lbb], writes=[fk])
            fw.op(fw.act, lambda: nc.scalar.activation(lf[:], fk[:], AF.Ln), reads=[fk], writes=[lf])
            fw.op(fw.dve, lambda: nc.vector.tensor_scalar(fk[:], fk[:], -1.0, 1.0, ALU.mult, ALU.add), reads=[fk], writes=[fk])
            for h in range(8):
                fw.mm(X[3], X[3][:, h * 4:h * 4 + 4], lf, lf[:, h * 128:(h + 1) * 128], ind, ind[:, 0:4], True, True)
            fw.op(fw.act, lambda: nc.scalar.activation(ebl[:], X[3][:, 0:32].rearrange("p (h c) -> p h c", c=4), AF.Exp),
                  reads=[X[3]], writes=[ebl])
            for n in range(2):
                fw.mm(X[3], X[3][:, n * 512:(n + 1) * 512], tri, tri[:], lf, lf[:, n * 512:(n + 1) * 512], True, True)
            for n in range(2):
                fw.mm(X[0], X[0][:, n * 512:(n + 1) * 512], bones, bones[:], lf, lf[:, n * 512:(n + 1) * 512], True, True)
            fw.op(fw.act, lambda: nc.scalar.copy(bc[:], X[3][:]), reads=[X[3]], writes=[bc])
            fw.op(fw.act, lambda: nc.scalar.activation(e1[:], bc[:], AF.Exp), reads=[bc], writes=[e1])
            fw.op(fw.dve, lambda: nc.vector.tensor_tensor(qs[:], qs[:], e1[:], ALU.mult), reads=[qs, e1], writes=[qs])
            fw.op(fw.act, lambda: nc.scalar.activation(e1[:], bc[:], AF.Exp, scale=-1.0), reads=[bc], writes=[e1])
            fw.op(fw.dve, lambda: nc.vector.tensor_tensor(kt[:], fk[:], e1[:], ALU.mult), reads=[fk, e1], writes=[kt])
            fw.op(fw.dve, lambda: nc.vector.tensor_tensor(bc[:], X[0][:], bc[:], ALU.subtract), reads=[X[0], bc], writes=[bc])
            fw.op(fw.act, lambda: nc.scalar.activation(e1[:], bc[:], AF.Exp), reads=[bc], writes=[e1])
            for c in range(4):
                fw.op(fw.dve, lambda: nc.vector.scalar_tensor_tensor(Kc[c][:], fk[:], ind[:, c:c + 1], e1[:], ALU.mult, ALU.mult),
                      reads=[fk, ind, e1], writes=[Kc[c]])
            for h in range(8):
                fw.op(fw.pe, lambda: nc.tensor.transpose(X[1][:, h * 128:(h + 1) * 128], qs[:, h * 128:(h + 1) * 128], self.ident[:]),
                      reads=[qs, self.ident], writes=[X[1]])
            x1v = X[1][:].rearrange("p (h t) -> p h t", t=128)
            fw.op(fw.act, lambda: nc.scalar.copy(qTb[:], x1v), reads=[X[1]], writes=[qTb])
            for c in range(4):
                fw.op(fw.dve, lambda: nc.vector.tensor_copy(Qc[c][:, :, 32 * c:32 * c + 32], x1v[:, :, 32 * c:32 * c + 32]),
                      reads=[X[1]], writes=[Qc[c]])
            for h in range(8):
                fw.op(fw.pe, lambda: nc.tensor.transpose(X[2][:, h * 128:(h + 1) * 128], kt[:, h * 128:(h + 1) * 128], self.ident[:]),
                      reads=[kt, self.ident], writes=[X[2]])
            fw.op(fw.act, lambda: nc.scalar.copy(kTb[:], X[2][:].rearrange("p (h t) -> p h t", t=128)), reads=[X[2]], writes=[kTb])
            for h in range(8):
                fw.mm(X[3], X[3][:, h * 128:(h + 1) * 128], kTb, kTb[:, h, :], qTb, qTb[:, h, :], True, True)
            fw.op(fw.dve, lambda: nc.vector.tensor_tensor(PT[:], X[3][:].rearrange("p (h t) -> p h t", t=128), M8[:], ALU.mult),
                  reads=[X[3], M8], writes=[PT])
            for h in range(8):
                fw.mm(X[0], X[0][:, h * 128:(h + 1) * 128], PT, PT[:, h, :], vb, vb[:, h * 128:(h + 1) * 128],
                      h % 4 == 0, False, skip_group_check=True)
            for ci, c in enumerate(corder):
                for h in range(8):
                    fw.mm(X[0], X[0][:, h * 128:(h + 1) * 128], Qc[c], Qc[c][:, h, :], Sb, Sb[:, h * 128:(h + 1) * 128],
                          False, ci == 3, skip_group_check=True)
                Xd = X[1 + (ci % 2)]
                for h in range(8):
                    fw.mm(Xd, Xd[:, h * 128:(h + 1) * 128], Kc[c], Kc[c][:, h * 128:(h + 1) * 128], vb, vb[:, h * 128:(h + 1) * 128],
                          True, True)
                for h in range(8):
                    fw.op(fw.dve, lambda: nc.vector.scalar_tensor_tensor(S[:, h * 128:(h + 1) * 128], S[:, h * 128:(h + 1) * 128],
                                                                         ebl[:, h, c:c + 1], Xd[:, h * 128:(h + 1) * 128], ALU.mult, ALU.add),
                          reads=[S, ebl, Xd], writes=[S])
                fw.op(fw.act, lambda: nc.scalar.copy(Sb[:], S[:]), reads=[S], writes=[Sb])
            o = ob[gi % 2]
            if dr == 0:
                fw.op(fw.act, lambda: nc.scalar.copy(o[:], X[0][:]), reads=[X[0]], writes=[o])
            else:
                fw.op(fw.dve, lambda: nc.vector.tensor_tensor(o[:], X[0][:], of[gi % 2][:], ALU.add), reads=[X[0], of[gi % 2]], writes=[o])
            fw.dma(fw.sp, self.OT[t0:t0 + 128, :], o[:], o, False)
        fw.barrier()
        fw.release(loc)
        st.close()

    def hgrn_readout(self, li, j, ctx_out):
        nc, fw = self.nc, self.fw
        st = contextlib.ExitStack()
        loc = []

        def sb(*a, **k):
            b = fw.sb(*a, stack=st, **k)
            loc.append(b)
            return b

        wg = sb([128, 8, 1024], BF16, "hgwg")
        wo = sb([128, 8, 1024], BF16, "hgwo")
        stg = [sb([128, 1024], F32, "hgstg") for _ in range(2)]
        self.load_w_bf(wg, self.hg_w_in[j].rearrange("(c p) f -> p c f", p=128), [4], stg)
        self.load_w_bf(wo, self.hg_w_out[j].rearrange("(c p) f -> p c f", p=128), [0], stg)
        gain = sb([128, 1024], F32, "gain")
        fw.dma(fw.sp, gain[:], self.hg_gain[j:j + 1, :].to_broadcast([128, D]), gain, True)
        uf = [sb([128, 8, 128], F32, "uf") for _ in range(2)]
        ot = [sb([128, 1024], F32, "ot") for _ in range(2)]
        ubf = sb([128, 8, 128], BF16, "ubf")
        sgt = sb([128, 1024], F32, "sgt")
        sq = sb([128, 1024], F32, "sq")
        ss = sb([128, 8], F32, "ss")
        r = sb([128, 1024], F32, "r")
        rT = sb([128, 8, 128], BF16, "rT")
        yo = [sb([128, 8, 128], F32, "yo") for _ in range(2)]
        X = [fw.ps([128, 1024], F32, "X", stack=st) for _ in range(3)]
        loc.extend(X)
        groups = ([L, L + 128] if ctx_out else []) + [g * 128 for g in range(64)]
        import os
        if os.environ.get("DBG_NG"):
            groups = groups[:int(os.environ["DBG_NG"])]

        def issue_load(gi):
            t0 = groups[gi]
            fw.dma(fw.sp, uf[gi % 2][:], self.UTv[:, :, t0:t0 + 128], uf[gi % 2], True)
            fw.dma(fw.sp, ot[gi % 2][:], self.OT[t0:t0 + 128, :], ot[gi % 2], True)

        issue_load(0)
        for gi, t0 in enumerate(groups):
            if gi + 1 < len(groups):
                issue_load(gi + 1)
            u = uf[gi % 2]
            o = ot[gi % 2]
            fw.op(fw.dve, lambda: nc.vector.tensor_copy(ubf[:], u[:]), reads=[u], writes=[ubf])
            self.proj_tm(X[0], ubf, wg, 0)
            fw.op(fw.act, lambda: nc.scalar.activation(sgt[:], X[0][:], AF.Silu), reads=[X[0]], writes=[sgt])
            fw.op(fw.dve, lambda: nc.vector.tensor_tensor(sq[:], o[:], o[:], ALU.mult), reads=[o], writes=[sq])
            fw.op(fw.dve, lambda: nc.vector.reduce_sum(ss[:], sq[:].rearrange("p (h e) -> p h e", e=128), AX.X), reads=[sq], writes=[ss])
            fw.op(fw.act, lambda: nc.scalar.activation(ss[:], ss[:], AF.Sqrt, bias=self.eps_t[:, 0:1], scale=1.0 / 128), reads=[ss, self.eps_t], writes=[ss])
            fw.op(fw.dve, lambda: nc.vector.reciprocal(ss[:], ss[:]), reads=[ss], writes=[ss])
            for h in range(8):
                fw.op(fw.act, lambda: nc.scalar.activation(r[:, h * 128:(h + 1) * 128], o[:, h * 128:(h + 1) * 128], AF.Copy, scale=ss[:, h:h + 1]),
                      reads=[o, ss], writes=[r])
            fw.op(fw.dve, lambda: nc.vector.tensor_tensor(r[:], r[:], gain[:], ALU.mult), reads=[r, gain], writes=[r])
            fw.op(fw.dve, lambda: nc.vector.tensor_tensor(r[:], r[:], sgt[:], ALU.mult), reads=[r, sgt], writes=[r])
            for h in range(8):
                fw.op(fw.pe, lambda: nc.tensor.transpose(X[1][:, h * 128:(h + 1) * 128], r[:, h * 128:(h + 1) * 128], self.ident[:]),
                      reads=[r, self.ident], writes=[X[1]])
            fw.op(fw.act, lambda: nc.scalar.copy(rT[:], X[1][:].rearrange("p (h t) -> p h t", t=128)), reads=[X[1]], writes=[rT])
            for dc in range(8):
                for jc in range(8):
                    fw.mm(X[2], X[2][:, dc * 128:(dc + 1) * 128], wo, wo[:, jc, dc * 128:(dc + 1) * 128], rT, rT[:, jc, :], jc == 0, jc == 7)
            y = yo[gi % 2]
            fw.op(fw.dve, lambda: nc.vector.tensor_copy(y[:], X[2][:].rearrange("p (c t) -> p c t", t=128)), reads=[X[2]], writes=[y])
            fw.dma(fw.sp, self.MTv[:, :, t0:t0 + 128], y[:], y, False)
        fw.barrier()
        fw.release(loc)
        st.close()


    def declare_hyena_inputs(self):
        self.hy_w_in = self.din("hy_w_in", [D, 3 * D])
        self.hy_w_out = self.din("hy_w_out", [D, D])
        self.hy_w1 = self.din("hy_w1", [33, 64])
        self.hy_w2 = self.din("hy_w2", [64, 64])
        self.hy_w3 = self.din("hy_w3", [64, 64])
        self.hy_w4 = self.din("hy_w4", [64, 2 * D])
        self.hy_sfb = self.din("hy_sfb", [64, 4])
        self.hy_feats = self.din("hy_feats", [128, NT])
        self.hy_tpos = self.din("hy_tpos", [1, NT])
        self.fftc_d = self.din("fftc", [128, 1920])
        self.ZT = self.scratch("ZT", [3 * D, NT])
        self.VT = self.scratch("VT", [D, NT])
        self.X0T = self.scratch("X0T", [D, NT])
        self.KT = self.scratch("KT", [2 * D, NT])
        self.KFT = self.scratch("KFT", [2, 128, D * 128])
        self.CT = self.scratch("CT", [D, NT])

    def phase_hyena(self, li):
        self.hy_proj()
        self.hy_shortconv()
        self.hy_filter()
        self.hy_fft(True)
        self.hy_fft(False)
        self.hy_ctxconv()
        self.hy_out()

    def _ctx(self):
        st = contextlib.ExitStack()
        loc = []
        fw = self.fw

        def sb(*a, **k):
            b = fw.sb(*a, stack=st, **k)
            loc.append(b)
            return b

        def ps(*a, **k):
            b = fw.ps(*a, stack=st, **k)
            loc.append(b)
            return b

        def done():
            fw.barrier()
            fw.release(loc)
            st.close()
        return sb, ps, done

    def hy_proj(self):
        nc, fw = self.nc, self.fw
        sb, ps, done = self._ctx()
        wb = sb([128, 8, 3072], BF16, "hyw")
        stg = [sb([128, 1024], F32, "hystg") for _ in range(2)]
        self.load_w_bf(wb, self.hy_w_in.rearrange("(c p) f -> p c f", p=128), [0, 1, 2], stg)
        uf = [sb([128, 8, 512], F32, "uf") for _ in range(2)]
        ubf = sb([128, 8, 512], BF16, "ubf")
        zo = [sb([128, 8, 512], F32, "zo") for _ in range(2)]
        pp = [ps([128, 512], F32, "pp") for _ in range(4)]
        ZTv = self.ZT.rearrange("(c p) t -> p c t", p=128)

        def issue_load(ti):
            w = 512 if ti < 16 else 256
            fw.dma(fw.sp, uf[ti % 2][:, :, 0:w], self.UTv[:, :, ti * 512:ti * 512 + w], uf[ti % 2], True)

        issue_load(0)
        n = 0
        for ti in range(17):
            w = 512 if ti < 16 else 256
            if ti + 1 < 17:
                issue_load(ti + 1)
            u = uf[ti % 2]
            fw.op(fw.dve, lambda: nc.vector.tensor_copy(ubf[:, :, 0:w], u[:, :, 0:w]), reads=[u], writes=[ubf])
            for k3 in range(3):
                z = zo[(ti * 3 + k3) % 2]
                for fc8 in range(8):
                    fc = k3 * 8 + fc8
                    p = pp[n % 4]
                    n += 1
                    for c in range(8):
                        fw.mm(p, p[:, 0:w], wb, wb[:, c, fc * 128:(fc + 1) * 128], ubf, ubf[:, c, 0:w], c == 0, c == 7)
                    fw.op(fw.act, lambda: nc.scalar.activation(z[:, fc8, 0:w], p[:, 0:w], AF.Identity, bias=self.vec("hy_b_in", fc), scale=1.0),
                          reads=[p, self.vecs], writes=[z])
                fw.dma(fw.sp, ZTv[:, k3 * 8:(k3 + 1) * 8, ti * 512:ti * 512 + w], z[:, :, 0:w], z, False)
        done()

    def hy_shortconv(self):
        nc, fw = self.nc, self.fw
        sb, ps, done = self._ctx()
        zr = [sb([128, NT], F32, "zr") for _ in range(2)]
        o = [sb([128, NT], F32, "o") for _ in range(2)]
        ZTv = self.ZT.rearrange("(c p) t -> p c t", p=128)
        VTv = self.VT.rearrange("(c p) t -> p c t", p=128)
        X0v = self.X0T.rearrange("(c p) t -> p c t", p=128)
        order = []
        for c in range(8):
            order += [(8 + c, 0), (16 + c, 1), (c, 0)]

        def issue_load(i):
            fw.dma(fw.sp, zr[i % 2][:], ZTv[:, order[i][0], :], zr[i % 2], True)

        issue_load(0)
        for i, (fc, oi) in enumerate(order):
            if i + 1 < len(order):
                issue_load(i + 1)
            z = zr[i % 2]
            ob = o[oi]
            w0, w1, w2, cb = (self.vec("hy_conv_w0", fc), self.vec("hy_conv_w1", fc), self.vec("hy_conv_w2", fc), self.vec("hy_conv_b", fc))
            fw.op(fw.act, lambda: nc.scalar.activation(ob[:], z[:], AF.Identity, bias=cb, scale=w1), reads=[z, self.vecs], writes=[ob])
            for (a, b_) in ((0, L), (L, NT)):
                fw.op(fw.dve, lambda: nc.vector.scalar_tensor_tensor(ob[:, a + 1:b_], z[:, a:b_ - 1], w0, ob[:, a + 1:b_], ALU.mult, ALU.add),
                      reads=[z, ob, self.vecs], writes=[ob])
                fw.op(fw.dve, lambda: nc.vector.scalar_tensor_tensor(ob[:, a:b_ - 1], z[:, a + 1:b_], w2, ob[:, a:b_ - 1], ALU.mult, ALU.add),
                      reads=[z, ob, self.vecs], writes=[ob])
            c = fc % 8
            if fc >= 16:
                fw.op(fw.dve, lambda: nc.vector.tensor_tensor(o[1][:], o[1][:], o[0][:], ALU.mult), reads=[o[0], o[1]], writes=[o[1]])
                fw.dma(fw.sp, VTv[:, c, :], o[1][:], o[1], False)
            elif fc < 8:
                fw.dma(fw.sp, X0v[:, c, :], o[0][:], o[0], False)
        done()

    def hy_filter(self):
        nc, fw = self.nc, self.fw
        sb, ps, done = self._ctx()
        w1 = sb([128, 128], F32, "w1")
        w2 = sb([128, 128], F32, "w2")
        w3 = sb([128, 128], F32, "w3")
        w4 = sb([128, 2 * D], F32, "w4")
        sfb = sb([128, 4], F32, "sfb")
        sfbb = sb([128, 3], F32, "sfbb")
        for t_ in (w1, w2, w3, w4, sfb):
            fw.op(fw.dve, lambda: nc.vector.memset(t_[:], 0.0), writes=[t_])
        fw.dma(fw.sp, w1[0:33, 0:64], self.hy_w1, w1, True)
        fw.dma(fw.sp, w2[0:64, 0:64], self.hy_w2, w2, True)
        fw.dma(fw.sp, w3[0:64, 0:64], self.hy_w3, w3, True)
        fw.dma(fw.sp, w4[0:64, :], self.hy_w4, w4, True)
        fw.dma(fw.sp, sfb[0:64, :], self.hy_sfb, sfb, True)
        fw.op(fw.dve, lambda: nc.vector.tensor_scalar(sfbb[:], sfb[:, 1:4], sfb[:, 0:1], None, ALU.mult), reads=[sfb], writes=[sfbb])
        ft = [sb([128, 512], F32, "ft") for _ in range(2)]
        tp = [sb([128, 512], F32, "tp") for _ in range(2)]
        hcur = [sb([128, 512], F32, "hc") for _ in range(2)]
        wr = sb([128, 512], F32, "wr")
        dec = [sb([128, 512], F32, "dec") for _ in range(2)]
        ko = [sb([128, 16, 512], F32, "ko") for _ in range(1)]
        pm = [ps([128, 512], F32, "pm") for _ in range(2)]
        pk = [ps([128, 512], F32, "pk") for _ in range(2)]
        KTv = self.KT.rearrange("(c p) t -> p c t", p=128)
        ws = [w1, w2, w3]

        def issue_load(ti):
            w = 512 if ti < 16 else 256
            fw.dma(fw.sp, ft[ti % 2][:, 0:w], self.hy_feats[:, ti * 512:ti * 512 + w], ft[ti % 2], True)
            fw.dma(fw.sp, tp[ti % 2][:, 0:w], self.hy_tpos[0:1, ti * 512:ti * 512 + w].to_broadcast([128, w]), tp[ti % 2], True)

        issue_load(0)
        for ti in range(17):
            w = 512 if ti < 16 else 256
            if ti + 1 < 17:
                issue_load(ti + 1)
            src_b, src_ap = ft[ti % 2], ft[ti % 2][:, 0:w]
            for l in range(3):
                p = pm[l % 2]
                kdim = 33 if l == 0 else 64
                fw.mm(p, p[:, 0:w], ws[l], ws[l][:], src_b, src_ap, True, True)
                h = hcur[l % 2]
                fw.op(fw.dve, lambda: nc.vector.tensor_scalar(h[:, 0:w], p[:, 0:w], sfb[:, 0:1], sfbb[:, l:l + 1], ALU.mult, ALU.add),
                      reads=[p, sfb, sfbb], writes=[h])
                fw.op(fw.dve, lambda: nc.vector.tensor_scalar(wr[:, 0:w], h[:, 0:w], PI, -2 * PI, ALU.is_gt, ALU.mult), reads=[h], writes=[wr])
                fw.op(fw.dve, lambda: nc.vector.tensor_tensor(h[:, 0:w], h[:, 0:w], wr[:, 0:w], ALU.add), reads=[h, wr], writes=[h])
                fw.op(fw.dve, lambda: nc.vector.tensor_scalar(wr[:, 0:w], h[:, 0:w], -PI, 2 * PI, ALU.is_lt, ALU.mult), reads=[h], writes=[wr])
                fw.op(fw.dve, lambda: nc.vector.tensor_tensor(h[:, 0:w], h[:, 0:w], wr[:, 0:w], ALU.add), reads=[h, wr], writes=[h])
                fw.op(fw.act, lambda: nc.scalar.activation(h[:, 0:w], h[:, 0:w], AF.Sin), reads=[h], writes=[h])
                src_b, src_ap = h, h[:, 0:w]
            k_ = ko[0]
            t_ = tp[ti % 2]
            for dd in range(16):
                p = pk[dd % 2]
                fw.mm(p, p[:, 0:w], w4, w4[:, dd * 128:(dd + 1) * 128], src_b, src_ap, True, True)
                de = dec[dd % 2]
                fw.op(fw.act, lambda: nc.scalar.activation(de[:, 0:w], t_[:, 0:w], AF.Exp, scale=self.vec("hy_negdelta", dd % 8)),
                      reads=[t_, self.vecs], writes=[de])
                fw.op(fw.dve, lambda: nc.vector.tensor_tensor(k_[:, dd, 0:w], p[:, 0:w], de[:, 0:w], ALU.mult), reads=[p, de], writes=[k_])
            if ti == 0 or ti == 16:
                fw.op(fw.dve, lambda: nc.vector.memset(k_[:, 8:16, 0:1], 0.0), writes=[k_])
            fw.dma(fw.sp, KTv[:, :, ti * 512:ti * 512 + w], k_[:, :, 0:w], k_, False)
        done()

    def fft_fwd(self, Xin, c0, fc, Y, Z, t1, t2, Ypr, Ypi):
        nc, fw = self.nc, self.fw
        for ci in range(4):
            fw.mm(Y, Y[:, ci, :], Xin, Xin[:, c0 + ci, :], fc, fc[:, 384:640], True, True)
        Yr = Y[:, :, 0:128]
        Yi = Y[:, :, 128:256]
        Tc = fc[:, 896:1408].rearrange("p (c k) -> p c k", k=128)
        Ts = fc[:, 1408:1920].rearrange("p (c k) -> p c k", k=128)
        fw.op(fw.dve, lambda: nc.vector.tensor_tensor(t1[:], Yr, Tc, ALU.mult), reads=[Y, fc], writes=[t1])
        fw.op(fw.dve, lambda: nc.vector.tensor_tensor(t2[:], Yi, Ts, ALU.mult), reads=[Y, fc], writes=[t2])
        fw.op(fw.dve, lambda: nc.vector.tensor_tensor(Ypr[:], t1[:], t2[:], ALU.add), reads=[t1, t2], writes=[Ypr])
        fw.op(fw.dve, lambda: nc.vector.tensor_tensor(t1[:], Yi, Tc, ALU.mult), reads=[Y, fc], writes=[t1])
        fw.op(fw.dve, lambda: nc.vector.tensor_tensor(t2[:], Yr, Ts, ALU.mult), reads=[Y, fc], writes=[t2])
        fw.op(fw.dve, lambda: nc.vector.tensor_tensor(Ypi[:], t1[:], t2[:], ALU.subtract), reads=[t1, t2], writes=[Ypi])
        C, S_, nS = fc[:, 128:256], fc[:, 256:384], fc[:, 0:128]
        yr = Ypr[:].rearrange("p c k -> p (c k)")
        yi = Ypi[:].rearrange("p c k -> p (c k)")
        fw.mm(Z[0], Z[0][:], fc, C, Ypr, yr, True, False)
        fw.mm(Z[0], Z[0][:], fc, S_, Ypi, yi, False, True)
        fw.mm(Z[1], Z[1][:], fc, C, Ypi, yi, True, False)
        fw.mm(Z[1], Z[1][:], fc, nS, Ypr, yr, False, True)

    def hy_fft(self, is_filter):
        nc, fw = self.nc, self.fw
        sb, ps, done = self._ctx()
        fc = sb([128, 1920], F32, "fftc")
        fw.dma(fw.sp, fc[:], self.fftc_d, fc, True)
        NSB = 16
        xin = [sb([128, NSB, 128], F32, "xin") for _ in range(2)]
        xin2 = [sb([128, NSB, 128], F32, "xin2") for _ in range(2)] if is_filter else None
        for t_ in xin + (xin2 or []):
            fw.op(fw.dve, lambda: nc.vector.memset(t_[:], 0.0), writes=[t_])
        kfr = [sb([128, NSB * 128], F32, "kfr") for _ in range(2)]
        kfi = [sb([128, NSB * 128], F32, "kfi") for _ in range(2)]
        xo = None if is_filter else [sb([64, NSB, 128], F32, "xo") for _ in range(2)]
        t1 = sb([128, 4, 128], F32, "t1")
        t2 = sb([128, 4, 128], F32, "t2")
        Ypr = sb([128, 4, 128], F32, "Ypr")
        Ypi = sb([128, 4, 128], F32, "Ypi")
        Ar = sb([128, 512], F32, "Ar")
        Ai = sb([128, 512], F32, "Ai")
        Wr = sb([128, 4, 128], F32, "Wr")
        Wi = sb([128, 4, 128], F32, "Wi")
        Y = ps([128, 4, 256], F32, "Y")
        Z = [ps([128, 512], F32, "Z") for _ in range(2)]
        if not is_filter:
            W = ps([128, 4, 256], F32, "W")
            XO = ps([128, 512], F32, "XO")
        KFv = self.KFT
        nsb = D // NSB
        import os
        if os.environ.get("DBG_NSB"):
            nsb = int(os.environ["DBG_NSB"])

        def issue_load(si):
            ch0 = si * NSB
            if is_filter:
                fw.dma(fw.sp, xin[si % 2][0:64], self.KT[ch0:ch0 + NSB, 0:L].rearrange("c (a b) -> a c b", b=128)[0:64], xin[si % 2], True)
                fw.dma(fw.sp, xin2[si % 2][0:64], self.KT[D + ch0:D + ch0 + NSB, 0:L].rearrange("c (a b) -> a c b", b=128)[0:64], xin2[si % 2], True)
            else:
                fw.dma(fw.sp, xin[si % 2][0:64], self.VT[ch0:ch0 + NSB, 0:L].rearrange("c (a b) -> a c b", b=128)[0:64], xin[si % 2], True)
                fw.dma(fw.sp, kfr[si % 2][:], KFv[0, :, ch0 * 128:(ch0 + NSB) * 128], kfr[si % 2], True)
                fw.dma(fw.sp, kfi[si % 2][:], KFv[1, :, ch0 * 128:(ch0 + NSB) * 128], kfi[si % 2], True)

        issue_load(0)
        for si in range(nsb):
            if si + 1 < nsb:
                issue_load(si + 1)
            ch0 = si * NSB
            X1 = xin[si % 2]
            kr, ki = kfr[si % 2], kfi[si % 2]
            for bi in range(NSB // 4):
                c0 = bi * 4
                sl = slice(c0 * 128, (c0 + 4) * 128)
                self.fft_fwd(X1, c0, fc, Y, Z, t1, t2, Ypr, Ypi)
                if is_filter:
                    fw.op(fw.act, lambda: nc.scalar.copy(Ar[:], Z[0][:]), reads=[Z[0]], writes=[Ar])
                    fw.op(fw.act, lambda: nc.scalar.copy(Ai[:], Z[1][:]), reads=[Z[1]], writes=[Ai])
                    self.fft_fwd(xin2[si % 2], c0, fc, Y, Z, t1, t2, Ypr, Ypi)
                    fw.op(fw.dve, lambda: nc.vector.tensor_tensor(kr[:, sl], Z[0][:], Ar[:], ALU.add), reads=[Z[0], Ar], writes=[kr])
                    fw.op(fw.dve, lambda: nc.vector.tensor_tensor(ki[:, sl], Ai[:], Z[1][:], ALU.subtract), reads=[Z[1], Ai], writes=[ki])
                    continue
                Pr = Ypr[:].rearrange("p c k -> p (c k)")
                Pi = Ypi[:].rearrange("p c k -> p (c k)")
                a1 = t1[:].rearrange("p c k -> p (c k)")
                a2 = t2[:].rearrange("p c k -> p (c k)")
                fw.op(fw.dve, lambda: nc.vector.tensor_tensor(a1, Z[0][:], kr[:, sl], ALU.mult), reads=[Z[0], kr], writes=[t1])
                fw.op(fw.dve, lambda: nc.vector.tensor_tensor(a2, Z[1][:], ki[:, sl], ALU.mult), reads=[Z[1], ki], writes=[t2])
                fw.op(fw.dve, lambda: nc.vector.tensor_tensor(Pr, a1, a2, ALU.subtract), reads=[t1, t2], writes=[Ypr])
                fw.op(fw.dve, lambda: nc.vector.tensor_tensor(a1, Z[0][:], ki[:, sl], ALU.mult), reads=[Z[0], ki], writes=[t1])
                fw.op(fw.dve, lambda: nc.vector.tensor_tensor(a2, Z[1][:], kr[:, sl], ALU.mult), reads=[Z[1], kr], writes=[t2])
                fw.op(fw.dve, lambda: nc.vector.tensor_tensor(Pi, a1, a2, ALU.add), reads=[t1, t2], writes=[Ypi])
                for ci in range(4):
                    fw.mm(W, W[:, ci, :], Ypr, Ypr[:, ci, :], fc, fc[:, 128:384], True, False)
                    fw.mm(W, W[:, ci, :], Ypi, Ypi[:, ci, :], fc, fc[:, 0:256], False, True)
                Wre = W[:, :, 0:128]
                Wim = W[:, :, 128:256]
                Tc = fc[:, 896:1408].rearrange("p (c k) -> p c k", k=128)
                Ts = fc[:, 1408:1920].rearrange("p (c k) -> p c k", k=128)
                fw.op(fw.dve, lambda: nc.vector.tensor_tensor(t1[:], Wre, Tc, ALU.mult), reads=[W, fc], writes=[t1])
                fw.op(fw.dve, lambda: nc.vector.tensor_tensor(t2[:], Wim, Ts, ALU.mult), reads=[W, fc], writes=[t2])
                fw.op(fw.dve, lambda: nc.vector.tensor_tensor(Wr[:], t1[:], t2[:], ALU.subtract), reads=[t1, t2], writes=[Wr])
                fw.op(fw.dve, lambda: nc.vector.tensor_tensor(t1[:], Wre, Ts, ALU.mult), reads=[W, fc], writes=[t1])
                fw.op(fw.dve, lambda: nc.vector.tensor_tensor(t2[:], Wim, Tc, ALU.mult), reads=[W, fc], writes=[t2])
                fw.op(fw.dve, lambda: nc.vector.tensor_tensor(Wi[:], t1[:], t2[:], ALU.add), reads=[t1, t2], writes=[Wi])
                fw.mm(XO, XO[:], fc, fc[:, 640:768], Wr, Wr[:].rearrange("p c k -> p (c k)"), True, False)
                fw.mm(XO, XO[:], fc, fc[:, 768:896], Wi, Wi[:].rearrange("p c k -> p (c k)"), False, True)
                o_ = xo[si % 2]
                fw.op(fw.act, lambda: nc.scalar.copy(o_[:, c0:c0 + 4, :], XO[0:64, :].rearrange("p (c k) -> p c k", k=128)), reads=[XO], writes=[o_])
            if is_filter:
                fw.dma(fw.sp, KFv[0, :, ch0 * 128:(ch0 + NSB) * 128], kr[:], kr, False)
                fw.dma(fw.sp, KFv[1, :, ch0 * 128:(ch0 + NSB) * 128], ki[:], ki, False)
            else:
                o_ = xo[si % 2]
                fw.dma(fw.sp, self.CT[ch0:ch0 + NSB, 0:L].rearrange("c (a b) -> a c b", b=128)[0:64], o_[:], o_, False)
        done()

    def hy_ctxconv(self):
        nc, fw = self.nc, self.fw
        sb, ps, done = self._ctx()
        VTv = self.VT.rearrange("(c p) t -> p c t", p=128)
        KTv = self.KT.rearrange("(c p) t -> p c t", p=128)
        CTv = self.CT.rearrange("(c p) t -> p c t", p=128)
        for c in range(8):
            vv = sb([128, LC], F32, "vv")
            kf = sb([128, LC], F32, "kf")
            kb = sb([128, LC], F32, "kb")
            acc = sb([128, LC], F32, "acc")
            fw.dma(fw.sp, vv[:], VTv[:, c, L:NT], vv, True)
            fw.dma(fw.sp, kf[:], KTv[:, c, L:NT], kf, True)
            fw.dma(fw.sp, kb[:], KTv[:, 8 + c, L:NT], kb, True)
            fw.op(fw.dve, lambda: nc.vector.tensor_scalar(acc[:], vv[:], kf[:, 0:1], None, ALU.mult), reads=[vv, kf], writes=[acc])
            for l in range(1, LC):
                fw.op(fw.dve, lambda: nc.vector.scalar_tensor_tensor(acc[:, l:LC], vv[:, 0:LC - l], kf[:, l:l + 1], acc[:, l:LC], ALU.mult, ALU.add),
                      reads=[vv, kf, acc], writes=[acc])
                fw.op(fw.dve, lambda: nc.vector.scalar_tensor_tensor(acc[:, 0:LC - l], vv[:, l:LC], kb[:, l:l + 1], acc[:, 0:LC - l], ALU.mult, ALU.add),
                      reads=[vv, kb, acc], writes=[acc])
            fw.dma(fw.sp, CTv[:, c, L:NT], acc[:], acc, False)
        done()

    def hy_out(self):
        nc, fw = self.nc, self.fw
        sb, ps, done = self._ctx()
        wo = sb([128, 8, 1024], BF16, "hywo")
        stg = [sb([128, 1024], F32, "hystg") for _ in range(2)]
        self.load_w_bf(wo, self.hy_w_out.rearrange("(c p) f -> p c f", p=128), [0], stg)
        x0 = [sb([128, 8, 512], F32, "x0") for _ in range(2)]
        vv = [sb([128, 8, 512], F32, "vv") for _ in range(2)]
        cv = [sb([128, 8, 512], F32, "cv") for _ in range(2)]
        yb = sb([128, 8, 512], BF16, "yb")
        yo = [sb([128, 8, 512], F32, "yo") for _ in range(2)]
        pp = [ps([128, 512], F32, "pp") for _ in range(4)]
        VTv = self.VT.rearrange("(c p) t -> p c t", p=128)
        X0v = self.X0T.rearrange("(c p) t -> p c t", p=128)
        CTv = self.CT.rearrange("(c p) t -> p c t", p=128)

        def issue_load(ti):
            w = 512 if ti < 16 else 256
            sl = slice(ti * 512, ti * 512 + w)
            fw.dma(fw.sp, x0[ti % 2][:, :, 0:w], X0v[:, :, sl], x0[ti % 2], True)
            fw.dma(fw.sp, vv[ti % 2][:, :, 0:w], VTv[:, :, sl], vv[ti % 2], True)
            fw.dma(fw.sp, cv[ti % 2][:, :, 0:w], CTv[:, :, sl], cv[ti % 2], True)

        issue_load(0)
        n = 0
        for ti in range(17):
            w = 512 if ti < 16 else 256
            if ti + 1 < 17:
                issue_load(ti + 1)
            a, b_, c_ = x0[ti % 2], vv[ti % 2], cv[ti % 2]
            for c in range(8):
                fw.op(fw.dve, lambda: nc.vector.scalar_tensor_tensor(c_[:, c, 0:w], b_[:, c, 0:w], self.vec("hy_filt_bias", c), c_[:, c, 0:w], ALU.mult, ALU.add),
                      reads=[b_, c_, self.vecs], writes=[c_])
            fw.op(fw.dve, lambda: nc.vector.tensor_tensor(yb[:, :, 0:w], c_[:, :, 0:w], a[:, :, 0:w], ALU.mult), reads=[a, c_], writes=[yb])
            y = yo[ti % 2]
            for dc in range(8):
                p = pp[n % 4]
                n += 1
                for jc in range(8):
                    fw.mm(p, p[:, 0:w], wo, wo[:, jc, dc * 128:(dc + 1) * 128], yb, yb[:, jc, 0:w], jc == 0, jc == 7)
                fw.op(fw.act, lambda: nc.scalar.activation(y[:, dc, 0:w], p[:, 0:w], AF.Identity, bias=self.vec("hy_b_out", dc), scale=1.0),
                      reads=[p, self.vecs], writes=[y])
            fw.dma(fw.sp, self.MTv[:, :, ti * 512:ti * 512 + w], y[:, :, 0:w], y, False)
        done()


    def phase_pool(self, li, ctx_live=True):
        nc, fw = self.nc, self.fw
        st = contextlib.ExitStack()
        loc = []

        def sb(*a, **k):
            b = fw.sb(*a, stack=st, **k)
            loc.append(b)
            return b

        A = sb([128, NT], F32, "plA")
        P = [sb([128, 144, 80], F32, "plP") for _ in range(2)]
        C = [sb([128, LC + 16], F32, "plC") for _ in range(2)]
        inv = sb([128, NT], F32, "plinv")
        PLb = sb([128, NT], BF16, "plb")
        for b in P + C:
            fw.op(fw.dve, lambda: nc.vector.memset(b[:], 0.0), writes=[b])
        WINS = (2, 4, 8, 16)

        def ranges(n, k):
            lo, hi = 1, n
            rs = [(lo, hi, 1, 0)]
            sh = 1
            for lev in range(2, k + 1):
                lo, hi = lo + sh, hi - sh
                rs.append((lo, hi, sh, sh))
                sh *= 2
            return rs

        for c in range(8):
            g = c // 2
            w = WINS[g]
            k = int(math.log2(w))
            fw.dma(fw.sp, A[:], self.UTv[:, c, :], A, True)
            if c % 2 == 0:
                fw.dma(fw.sp, inv[:], self.pool_inv[g:g + 1, :].to_broadcast([128, NT]), inv, True)
            X = P[0]
            if c > 0:
                fw.op(fw.dve, lambda: nc.vector.memset(X[:, :, 0:8], 0.0), writes=[X])
                fw.op(fw.dve, lambda: nc.vector.memset(X[:, :, 72:80], 0.0), writes=[X])
                fw.op(fw.dve, lambda: nc.vector.memset(X[:, 0:8, :], 0.0), writes=[X])
                fw.op(fw.dve, lambda: nc.vector.memset(X[:, 136:144, :], 0.0), writes=[X])
            fw.op(fw.act, lambda: nc.scalar.copy(X[:, 8:136, 8:72], A[:, 0:L].rearrange("p (r q) -> p r q", q=64)),
                  reads=[A], writes=[X])
            cur = 0
            for (lo, hi, s0, s1) in ranges(80, k):
                src, dst = P[cur], P[1 - cur]
                fw.op(fw.dve, lambda: nc.vector.tensor_tensor(dst[:, :, lo:hi], src[:, :, lo - s0:hi - s0], src[:, :, lo + s1:hi + s1], ALU.add),
                      reads=[src], writes=[dst])
                cur = 1 - cur
            for (lo, hi, s0, s1) in ranges(144, k):
                src, dst = P[cur], P[1 - cur]
                fw.op(fw.dve, lambda: nc.vector.tensor_tensor(dst[:, lo:hi, 8:72], src[:, lo - s0:hi - s0, 8:72], src[:, lo + s1:hi + s1, 8:72], ALU.add),
                      reads=[src], writes=[dst])
                cur = 1 - cur
            S = P[cur]
            Sint = S[:, 8:136, 8:72]
            fw.op(fw.dve, lambda: nc.vector.tensor_tensor(Sint, Sint, inv[:, 0:L].rearrange("p (r q) -> p r q", q=64), ALU.mult),
                  reads=[S, inv], writes=[S])
            fw.op(fw.dve, lambda: nc.vector.tensor_tensor(PLb[:, 0:L].rearrange("p (r q) -> p r q", q=64), Sint,
                                                          A[:, 0:L].rearrange("p (r q) -> p r q", q=64), ALU.subtract),
                  reads=[S, A], writes=[PLb])
            X1 = C[0]
            if c > 0:
                fw.op(fw.dve, lambda: nc.vector.memset(X1[:, 0:8], 0.0), writes=[X1])
                fw.op(fw.dve, lambda: nc.vector.memset(X1[:, 8 + LC:16 + LC], 0.0), writes=[X1])
            fw.op(fw.act, lambda: nc.scalar.copy(X1[:, 8:8 + LC], A[:, L:NT]), reads=[A], writes=[X1])
            cur = 0
            for (lo, hi, s0, s1) in ranges(LC + 16, k):
                src, dst = C[cur], C[1 - cur]
                fw.op(fw.dve, lambda: nc.vector.tensor_tensor(dst[:, lo:hi], src[:, lo - s0:hi - s0], src[:, lo + s1:hi + s1], ALU.add),
                      reads=[src], writes=[dst])
                cur = 1 - cur
            S1 = C[cur]
            fw.op(fw.dve, lambda: nc.vector.tensor_tensor(S1[:, 8:8 + LC], S1[:, 8:8 + LC], inv[:, L:NT], ALU.mult),
                  reads=[S1, inv], writes=[S1])
            fw.op(fw.dve, lambda: nc.vector.tensor_tensor(PLb[:, L:NT], S1[:, 8:8 + LC], A[:, L:NT], ALU.subtract),
                  reads=[S1, A], writes=[PLb])
            fw.dma(fw.sp, self.PLTv[:, c, :], PLb[:], PLb, False)
        fw.barrier()
        fw.release(loc)
        st.close()

        st = contextlib.ExitStack()
        loc = []
        wst = sb([128, 8, 256], F32, "plws")
        wpb = sb([128, 8, 256], BF16, "plw")
        plt = [sb([128, 8, 512], BF16, "plt") for _ in range(2)]
        yo = [sb([128, 8, 512], F32, "plyo") for _ in range(2)]
        pp = [fw.ps([128, 512], F32, "plp", stack=st) for _ in range(4)]
        loc.extend(pp)
        fw.dma(fw.sp, wst[:], self.pool_w.rearrange("g (cc p) e -> p (g cc) e", p=128), wst, True)
        fw.op(fw.dve, lambda: nc.vector.tensor_copy(wpb[:], wst[:]), reads=[wst], writes=[wpb])
        ntiles = 17 if ctx_live else 16

        def issue_load(ti):
            w = 512 if ti < 16 else 256
            fw.dma(fw.sp, plt[ti % 2][:, :, 0:w], self.PLTv[:, :, ti * 512:ti * 512 + w], plt[ti % 2], True)

        issue_load(0)
        n = 0
        for ti in range(ntiles):
            w = 512 if ti < 16 else 256
            if ti + 1 < ntiles:
                issue_load(ti + 1)
            pl = plt[ti % 2]
            y = yo[ti % 2]
            for g in range(4):
                for ec in range(2):
                    p = pp[n % 4]
                    n += 1
                    for cc in range(2):
                        fw.mm(p, p[:, 0:w], wpb, wpb[:, g * 2 + cc, ec * 128:(ec + 1) * 128], pl, pl[:, 2 * g + cc, 0:w], cc == 0, cc == 1)
                    oc = 2 * g + ec
                    if n % 2 == 0:
                        fw.op(fw.dve, lambda: nc.vector.tensor_scalar(y[:, oc, 0:w], p[:, 0:w], self.vec("pool_scale", oc), None, ALU.mult),
                              reads=[p, self.vecs], writes=[y])
                    else:
                        fw.op(fw.act, lambda: nc.scalar.activation(y[:, oc, 0:w], p[:, 0:w], AF.Copy, scale=self.vec("pool_scale", oc)),
                              reads=[p, self.vecs], writes=[y])
            fw.dma(fw.sp, self.MTv[:, :, ti * 512:ti * 512 + w], y[:, :, 0:w], y, False)
        fw.barrier()
        fw.release(loc)
        st.close()

    def phase_out(self):
        nc, fw = self.nc, self.fw
        st = contextlib.ExitStack()
        loc = []

        def sb(*a, **k):
            b = fw.sb(*a, stack=st, **k)
            loc.append(b)
            return b

        hb2 = [sb([128, 8, 512], F32, "h") for _ in range(2)]
        sq = sb([128, 8, 512], BF16, "sq")
        rstd = sb([128, 512], F32, "rstd")
        hn = sb([128, 8, 512], F32, "hn")
        ob = [sb([128, 4, D], F32, "ob") for _ in range(2)]
        ps_stat = fw.ps([128, 512], F32, "pstat", stack=st)
        pt = [fw.ps([128, 8, 128], F32, "ptr", stack=st) for _ in range(2)]
        loc.extend([ps_stat] + pt)
        npt = 0
        fw.dma(fw.sp, hb2[0][:], self.HTv[:, :, 0:512], hb2[0], True)
        for ti in range(16):
            c0 = ti * 512
            hb = hb2[ti % 2]
            o = ob[ti % 2]
            if ti + 1 < 16:
                fw.dma(fw.sp, hb2[(ti + 1) % 2][:], self.HTv[:, :, c0 + 512:c0 + 1024], hb2[(ti + 1) % 2], True)
            self.rms_stats(hb, 512, sq, ps_stat, rstd)
            for c in range(8):
                fw.op(fw.dve, lambda: nc.vector.scalar_tensor_tensor(hn[:, c, :], hb[:, c, :], self.vec("final_gain", c),
                                                                     rstd[:], ALU.mult, ALU.mult),
                      reads=[hb, rstd, self.vecs], writes=[hn])
            for g in range(4):
                p = pt[npt % 2]
                npt += 1
                for c in range(8):
                    fw.op(fw.pe, lambda: nc.tensor.transpose(p[:, c, :], hn[:, c, g * 128:(g + 1) * 128], self.ident[:]),
                          reads=[hn, self.ident], writes=[p])
                if g % 2 == 0:
                    fw.op(fw.dve, lambda: nc.vector.tensor_copy(o[:, g, :], p[:].rearrange("p c t -> p (c t)")), reads=[p], writes=[o])
                else:
                    fw.op(fw.act, lambda: nc.scalar.copy(o[:, g, :], p[:].rearrange("p c t -> p (c t)")), reads=[p], writes=[o])
            fw.dma(fw.sp, self.outT[c0:c0 + 512, :].rearrange("(g p) d -> p g d", p=128), o[:], o, False)
        fw.barrier()
        fw.release(loc)
        st.close()


def _pc(v):
    v = np.asarray(v, np.float32).reshape(-1, 128)
    return np.ascontiguousarray(v.T)


def _window_bounds(n, w):
    pos = np.arange(n)
    return np.clip(pos - w // 2, 0, n), np.clip(pos + (w - w // 2), 0, n)


def host_consts():
    c = {}
    c["ident"] = np.eye(128, dtype=np.float32)
    inv = np.zeros((4, NT), np.float32)
    for gi, w in enumerate((2, 4, 8, 16)):
        rlo, rhi = _window_bounds(128, w)
        clo, chi = _window_bounds(64, w)
        cnt = ((rhi - rlo)[:, None] * (chi - clo)[None, :]).astype(np.float32)
        inv[gi, :L] = (1.0 / cnt).reshape(-1)
        lo, hi = _window_bounds(LC, w)
        inv[gi, L:] = 1.0 / (hi - lo).astype(np.float32)
    c["pool_inv"] = inv
    tri = np.zeros((4, 128, 128), np.float32)
    blk = np.arange(128) // 32
    same = blk[:, None] == blk[None, :]
    sidx = np.arange(128)[:, None]
    tidx = np.arange(128)[None, :]
    tri[0] = (same & (sidx <= tidx)).astype(np.float32)
    tri[1] = (same & (sidx >= tidx)).astype(np.float32)
    tri[2] = same.astype(np.float32)
    for c_ in range(4):
        tri[3, :, c_] = (blk == c_).astype(np.float32)
    c["tri"] = tri
    def feats(Lx):
        pos = np.arange(Lx, dtype=np.float32)
        t = np.linspace(0.0, 1.0, Lx, dtype=np.float32)
        bands = np.linspace(1e-4, 15, 16, dtype=np.float32)
        ang = (np.float32(2 * math.pi) * pos / np.float32(Lx))[:, None] * bands[None, :]
        return np.concatenate([t[:, None], np.cos(ang), -np.sin(ang)], axis=1).astype(np.float32), t
    fl, tl = feats(L)
    fcx, tcx = feats(LC)
    hf_ = np.zeros((128, NT), np.float32)
    hf_[:33] = np.concatenate([fl, fcx], axis=0).T
    c["hy_feats"] = hf_
    c["hy_tpos"] = np.concatenate([tl, tcx])[None, :].astype(np.float32)
    deltas = np.abs(np.linspace(math.log(1e-2) / 1.5, math.log(1e-2) / 0.3, D, dtype=np.float32))
    c["negdelta"] = (-deltas).astype(np.float32)
    a = np.arange(128, dtype=np.float64)
    ang = 2 * np.pi * np.outer(a, a) / 128.0
    C, S = np.cos(ang), np.sin(ang)
    angN = 2 * np.pi * np.outer(a, a) / 16384.0
    Tc, Ts = np.cos(angN), np.sin(angN)
    fftc = np.concatenate([-S, C, S, C, -S, C / 16384.0, -S / 16384.0, np.tile(Tc, (1, 4)), np.tile(Ts, (1, 4))], axis=1)
    c["fftc"] = fftc.astype(np.float32)
    return c


def make_in_maps(inputs, ncores=NCORES):
    f = lambda k: np.ascontiguousarray(np.asarray(inputs[k], np.float32))
    consts = host_consts()
    vecs = np.zeros((128, NV), np.float32)

    def put(name, v):
        off, n = VEC_LAY[name]
        vecs[:, off:off + n] = _pc(v)

    put("final_gain", f("final_gain"))
    for j in range(2):
        put("hg_gain%d" % j, f("hg_norm_gain")[j])
    put("pool_scale", f("pool_scale")[0])
    put("hy_b_in", f("hy_b_in")[0])
    for t in range(3):
        put("hy_conv_w%d" % t, f("hy_conv_w")[0, t])
    put("hy_conv_b", f("hy_conv_b")[0])
    put("hy_filt_bias", f("hy_filt_bias")[0])
    put("hy_b_out", f("hy_b_out")[0])
    put("hy_negdelta", consts["negdelta"])
    shared = {
        "w_ada": f("w_ada"), "b_ada": f("b_ada"),
        "ffn_w_gate": f("ffn_w_gate"), "ffn_w_up": f("ffn_w_up"), "ffn_w_down": f("ffn_w_down"),
        "vecs": vecs, "ident": consts["ident"], "pool_w": f("pool_w")[0], "pool_inv": consts["pool_inv"],
        "hg_w_in": f("hg_w_in"), "hg_w_out": f("hg_w_out"), "hg_lb_logits": f("hg_lb_logits"), "hg_norm_gain": f("hg_norm_gain"),
        "tri": consts["tri"],
        "hy_w_in": f("hy_w_in")[0], "hy_w_out": f("hy_w_out")[0], "hy_w1": f("hy_w1")[0], "hy_w2": f("hy_w2")[0],
        "hy_w3": f("hy_w3")[0], "hy_w4": f("hy_w4")[0],
        "hy_sfb": np.ascontiguousarray(np.stack([f("hy_sin_freq")[0], f("hy_b1")[0], f("hy_b2")[0], f("hy_b3")[0]], axis=1)),
        "hy_feats": consts["hy_feats"], "hy_tpos": consts["hy_tpos"], "fftc": consts["fftc"],
    }
    maps = []
    x = f("x")
    ctx = f("ctx")
    c = f("c")
    cc = f("c_ctx")
    for i in range(ncores):
        b = i % 4
        m = dict(shared)
        m["x"] = x[b]
        m["ctx"] = ctx[b]
        cs = np.stack([c[b], cc], axis=-1)
        m["csT"] = np.ascontiguousarray(cs.reshape(8, 128, 2).transpose(1, 0, 2))
        maps.append(m)
    return maps


_PROG_CACHE = {}


def kernel(**inputs):
    if "main" not in _PROG_CACHE:
        _PROG_CACHE["main"] = Prog().build()
    nc = _PROG_CACHE["main"]
    maps = make_in_maps(inputs)
    res = run_bass_kernel_spmd(nc, maps, core_ids=list(range(NCORES)))
    out = np.stack([np.asarray(res.results[b]["out"], np.float32) for b in range(4)], axis=0)
    return out
```

```python
import contextlib
import math
import numpy as np
import concourse.bass as bass
import concourse.mybir as mybir
from concourse.bass_utils import run_bass_kernel_spmd

F32 = mybir.dt.float32
BF16 = mybir.dt.bfloat16
F32R = mybir.dt.float32r
ALU = mybir.AluOpType
AF = mybir.ActivationFunctionType
AX = mybir.AxisListType

D = 1024
L = 8192
LC = 256
NT = L + LC
DFF = 2816
DEPTH = 4
EPS = 1e-6
NCORES = 4
PI = float(np.pi)


class Buf:
    def __init__(self, t, name):
        self.t = t
        self.name = name
        self.last_w = None
        self.readers = []
        self.dsem = {}
        self.dcount = {}

    def __getitem__(self, k):
        return self.t[k]


class Eng:
    def __init__(self, fw, eng, name):
        self.eng = eng
        self.name = name
        self.sem = fw.new_sem("c_" + name)
        self.count = 0
        self.known = {}


class FW:
    def __init__(self, nc):
        self.nc = nc
        self.es = contextlib.ExitStack()
        self.nsem = 0
        self.pe = Eng(self, nc.tensor, "pe")
        self.act = Eng(self, nc.scalar, "act")
        self.dve = Eng(self, nc.vector, "dve")
        self.pool = Eng(self, nc.gpsimd, "pool")
        self.sp = Eng(self, nc.sync, "sp")
        self.engs = [self.pe, self.act, self.dve, self.pool, self.sp]
        self.bufs = []
        self.ack = self.new_sem("ack")
        self.clr = self.new_sem("clr")
        self.bar_n = 0
        self.dma_events = []
        self.dsem_pool = []
        self.nbuf = 0

    def new_sem(self, name):
        s = self.es.enter_context(self.nc.semaphore(name + "_%d" % self.nsem))
        self.nsem += 1
        return s

    def sb(self, shape, dt=F32, name="sb", stack=None):
        self.nbuf += 1
        t = (stack or self.es).enter_context(self.nc.sbuf_tensor("%s_%d" % (name, self.nbuf), list(shape), dt))
        b = Buf(t, name)
        self.bufs.append(b)
        return b

    def ps(self, shape, dt=F32, name="ps", stack=None):
        self.nbuf += 1
        t = (stack or self.es).enter_context(self.nc.psum_tensor("%s_%d" % (name, self.nbuf), list(shape), dt))
        b = Buf(t, name)
        self.bufs.append(b)
        return b

    def release(self, bufs):
        for b in bufs:
            for k, s_ in b.dsem.items():
                self.dsem_pool.append(s_)
            b.dsem = {}
            b.dcount = {}
            if b in self.bufs:
                self.bufs.remove(b)

    def _wait(self, e, ev):
        if ev is None:
            return
        sem, val, src = ev
        if src is e and e is self.pe:
            return
        key = id(sem)
        if e.known.get(key, 0) >= val:
            return
        e.eng.wait_ge(sem, val)
        e.known[key] = val

    def _deps(self, e, reads, writes):
        for b in reads:
            self._wait(e, b.last_w)
        for b in writes:
            self._wait(e, b.last_w)
            for r in b.readers:
                self._wait(e, r)

    def op(self, e, fn, reads=(), writes=()):
        self._deps(e, reads, writes)
        ins = fn()
        e.count += 1
        ins.then_inc(e.sem, 1)
        ev = (e.sem, e.count, e)
        for b in reads:
            b.readers.append(ev)
            if len(b.readers) > 16:
                latest = {}
                for r in b.readers:
                    k = id(r[0])
                    if k not in latest or latest[k][1] < r[1]:
                        latest[k] = r
                b.readers = list(latest.values())
        for b in writes:
            b.last_w = ev
            b.readers = []
        return ins

    def dma(self, e, out, in_, buf, is_load, **kw):
        if e.name not in buf.dsem:
            buf.dsem[e.name] = self.dsem_pool.pop() if self.dsem_pool else self.new_sem("d")
            buf.dcount[e.name] = 0
        if is_load:
            self._deps(e, [], [buf])
        else:
            self._deps(e, [buf], [])
        ins = e.eng.dma_start(out=out, in_=in_, **kw)
        buf.dcount[e.name] += 1
        ins.then_inc(buf.dsem[e.name], 16)
        ev = (buf.dsem[e.name], 16 * buf.dcount[e.name], None)
        if is_load:
            buf.last_w = ev
            buf.readers = []
        else:
            buf.readers.append(ev)
        self.dma_events.append(ev)
        return ins

    def barrier(self):
        evs = [(x.sem, x.count, x) for x in self.engs if x.count > 0]
        dm = {}
        for (s, v, _) in self.dma_events:
            if id(s) not in dm or dm[id(s)][1] < v:
                dm[id(s)] = (s, v, None)
        for e in self.engs:
            for ev in evs:
                if ev[2] is e:
                    continue
                self._wait(e, ev)
            for ev in dm.values():
                self._wait(e, ev)
        self.dma_events = []
        for b in self.bufs:
            b.last_w = None
            b.readers = []
        n = len(self.engs)
        self.bar_n += 1
        for e in self.engs:
            e.eng.sem_inc(self.ack, 1)
        for e in self.engs:
            e.eng.wait_ge(self.ack, n * self.bar_n)
            e.eng.sem_clear(e.sem)
            e.count = 0
            if e is self.sp:
                for b in self.bufs:
                    for k in b.dsem:
                        if b.dcount[k] > 0:
                            e.eng.sem_clear(b.dsem[k])
                            b.dcount[k] = 0
            e.eng.sem_inc(self.clr, 1)
        for e in self.engs:
            e.eng.wait_ge(self.clr, n * self.bar_n)
            e.known = {}

    def mm(self, out_b, out_ap, lhsT_b, lhsT_ap, rhs_b, rhs_ap, start, stop, r32=False, **kw):
        nc = self.nc
        if r32:
            lhsT_ap = lhsT_ap.bitcast(F32R)
            rhs_ap = rhs_ap.bitcast(F32R)
        return self.op(self.pe, lambda: nc.tensor.matmul(out_ap, lhsT_ap, rhs_ap, start=start, stop=stop, **kw),
                       reads=[lhsT_b, rhs_b], writes=[out_b])


VEC_SLOTS = {}


def _vec_layout():
    off = 0
    lay = {}

    def add(name, n):
        nonlocal off
        lay[name] = (off, n // 128)
        off += n // 128

    add("final_gain", D)
    for j in range(2):
        add("hg_gain%d" % j, D)
    add("pool_scale", D)
    add("hy_b_in", 3 * D)
    for t in range(3):
        add("hy_conv_w%d" % t, 3 * D)
    add("hy_conv_b", 3 * D)
    add("hy_filt_bias", D)
    add("hy_b_out", D)
    add("hy_negdelta", D)
    return lay, off


VEC_LAY, NV = _vec_layout()


class Prog:
    def __init__(self, dbg=None, stop_after=None):
        self.dbg = dbg or []
        self.stop_after = stop_after
        nc = bass.Bass("TRN2", target_bir_lowering=False)
        self.nc = nc
        self.fw = FW(nc)
        self.inp = {}
        self.out = None

    def din(self, name, shape, dt=F32):
        t = self.nc.dram_tensor(name, list(shape), dt, kind="ExternalInput").ap()
        self.inp[name] = t
        return t

    def scratch(self, name, shape, dt=F32):
        if name in self.dbg:
            return self.nc.dram_tensor(name, list(shape), dt, kind="ExternalOutput").ap()
        return self.nc.dram_tensor(name, list(shape), dt).ap()

    def build(self):
        nc, fw = self.nc, self.fw
        x = self.din("x", [L, D])
        ctx = self.din("ctx", [LC, D])
        csT = self.din("csT", [128, 8, 2])
        w_ada = self.din("w_ada", [DEPTH, D, 9 * D])
        b_ada = self.din("b_ada", [DEPTH, 9 * D])
        self.wg = self.din("ffn_w_gate", [DEPTH, 2, D, DFF])
        self.wu = self.din("ffn_w_up", [DEPTH, 2, D, DFF])
        self.wd = self.din("ffn_w_down", [DEPTH, 2, DFF, D])
        vecs = self.din("vecs", [128, NV])
        ident = self.din("ident", [128, 128])
        self.pool_w = self.din("pool_w", [4, 256, 256])
        self.pool_inv = self.din("pool_inv", [4, NT])
        self.declare_mixer_inputs()
        self.outT = self.nc.dram_tensor("out", [L, D], F32, kind="ExternalOutput").ap()

        self.HT = self.scratch("HT", [D, NT])
        self.UT = self.scratch("UT", [D, NT])
        self.YT = self.scratch("YT", [D, NT], BF16)
        self.MT = self.scratch("MT", [D, NT])
        self.MTv = self.MT.rearrange("(c p) t -> p c t", p=128)
        self.PLT = self.scratch("PLT", [D, NT], BF16)
        self.PLTv = self.PLT.rearrange("(c p) t -> p c t", p=128)
        self.HTv = self.HT.rearrange("(c p) t -> p c t", p=128)
        self.UTv = self.UT.rearrange("(c p) t -> p c t", p=128)
        self.YTv = self.YT.rearrange("(c p) t -> p c t", p=128)

        self.ident = fw.sb([128, 128], F32, "ident")
        self.vecs = fw.sb([128, NV], F32, "vecs")
        self.mod = fw.sb([128, DEPTH, 72, 2], F32, "mod")
        self.ones_bf = fw.sb([128, 128], BF16, "ones_bf")
        self.eps_t = fw.sb([128, 1], F32, "eps")
        fw.dma(fw.sp, self.ident[:], ident, self.ident, True)
        fw.dma(fw.sp, self.vecs[:], vecs, self.vecs, True)
        fw.op(fw.dve, lambda: nc.vector.memset(self.ones_bf[:], 1.0 / D), writes=[self.ones_bf])
        fw.op(fw.dve, lambda: nc.vector.memset(self.eps_t[:], EPS), writes=[self.eps_t])

        self.phase_mods(csT, w_ada, b_ada)
        if self.stop_after == "mods":
            return self.finish()
        self.phase_in(x, ctx)
        if self.stop_after == "in":
            return self.finish()
        for li in range(DEPTH):
            kind = li % 3
            last = li == DEPTH - 1
            self.phase_ffn(li, 0, 0, make_u=False, ctx_live=True)
            if self.stop_after == "ffnA_%d" % li:
                return self.finish()
            self.phase_ffn(li, 0, 1, make_u=True, ctx_live=True)
            if self.stop_after == "ffn1_%d" % li:
                return self.finish()
            if kind == 1:
                self.phase_pool(li)
            elif kind == 0:
                self.phase_hgrn(li)
            else:
                self.phase_hyena(li)
            if self.stop_after == "mix_%d" % li:
                return self.finish()
            self.phase_ffn(li, 1, 0, make_u=False, ctx_live=not last)
            self.phase_ffn(li, 1, 1, make_u=False, ctx_live=not last)
            if self.stop_after == "ffn2_%d" % li:
                return self.finish()
        self.phase_out()
        return self.finish()

    def build_mixer_test(self, li):
        nc, fw = self.nc, self.fw
        vecs = self.din("vecs", [128, NV])
        ident = self.din("ident", [128, 128])
        self.pool_w = self.din("pool_w", [4, 256, 256])
        self.pool_inv = self.din("pool_inv", [4, NT])
        self.UT = self.din("UT", [D, NT])
        self.MT = self.nc.dram_tensor("MT", [D, NT], F32, kind="ExternalOutput").ap()
        self.UTv = self.UT.rearrange("(c p) t -> p c t", p=128)
        self.MTv = self.MT.rearrange("(c p) t -> p c t", p=128)
        self.PLT = self.scratch("PLT", [D, NT], BF16)
        self.PLTv = self.PLT.rearrange("(c p) t -> p c t", p=128)
        self.ident = fw.sb([128, 128], F32, "ident")
        self.vecs = fw.sb([128, NV], F32, "vecs")
        fw.dma(fw.sp, self.ident[:], ident, self.ident, True)
        fw.dma(fw.sp, self.vecs[:], vecs, self.vecs, True)
        self.eps_t = fw.sb([128, 1], F32, "eps")
        fw.op(fw.dve, lambda: nc.vector.memset(self.eps_t[:], EPS), writes=[self.eps_t])
        self.mixer_inputs()
        kind = li % 3
        if kind == 1:
            self.phase_pool(li)
        elif kind == 0:
            self.phase_hgrn(li)
        else:
            self.phase_hyena(li)
        return self.finish()

    def mixer_inputs(self):
        pass

    def finish(self):
        if "MODD" in self.dbg:
            md = self.nc.dram_tensor("MODD", [128, DEPTH * 72 * 2], F32, kind="ExternalOutput").ap()
            self.fw.dma(self.fw.sp, md, self.mod[:].rearrange("p l j m -> p (l j m)"), self.mod, False)
        self.fw.barrier()
        self.fw.es.close()
        return self.nc

    def vec(self, name, c):
        off, n = VEC_LAY[name]
        return self.vecs[:, off + c:off + c + 1]

    def modv(self, li, k, c, m):
        return self.mod[:, li, k * 8 + c, m:m + 1]

    def phase_mods(self, csT, w_ada, b_ada):
        nc, fw = self.nc, self.fw
        st = contextlib.ExitStack()
        loc = []

        def sb(*a, **k):
            b = fw.sb(*a, stack=st, **k)
            loc.append(b)
            return b

        cs = sb([128, 8, 2], F32, "cs")
        ones2 = sb([1, 2], F32, "ones2")
        brow = sb([1, 9 * D], F32, "brow")
        wbuf = [sb([128, 8, 512], F32, "wada") for _ in range(2)]
        pm = fw.ps([128, 144], F32, "pm", stack=st)
        loc.append(pm)
        fw.dma(fw.sp, cs[:], csT, cs, True)
        fw.op(fw.act, lambda: nc.scalar.activation(cs[:], cs[:], AF.Silu), reads=[cs], writes=[cs])
        fw.op(fw.dve, lambda: nc.vector.memset(ones2[:], 1.0), writes=[ones2])
        n = 0
        for li in range(DEPTH):
            fw.dma(fw.sp, brow[:], b_ada[li:li + 1, :], brow, True)
            wv = w_ada[li].rearrange("(kc p) f -> p kc f", p=128)
            for piece in range(18):
                wb = wbuf[n % 2]
                n += 1
                fw.dma(fw.sp, wb[:], wv[:, :, piece * 512:(piece + 1) * 512], wb, True)
                for jj in range(4):
                    j = piece * 4 + jj
                    for kc in range(8):
                        fw.mm(pm, pm[:, 2 * j:2 * j + 2], wb, wb[:, kc, jj * 128:(jj + 1) * 128], cs, cs[:, kc, :],
                              kc == 0, False)
                    fw.mm(pm, pm[:, 2 * j:2 * j + 2], brow, brow[0:1, j * 128:(j + 1) * 128], ones2, ones2[0:1, :],
                          False, True)
            fw.op(fw.dve, lambda: nc.vector.tensor_copy(self.mod[:, li, :, :], pm[:].rearrange("p (j m) -> p j m", m=2)),
                  reads=[pm], writes=[self.mod])
            for k in (1, 4, 7):
                sl = self.mod[:, li, k * 8:(k + 1) * 8, :]
                fw.op(fw.dve, lambda: nc.vector.tensor_scalar(sl, sl, 1.0, None, ALU.add), reads=[self.mod], writes=[self.mod])
            for k in (2, 8):
                sl = self.mod[:, li, k * 8:(k + 1) * 8, :]
                fw.op(fw.dve, lambda: nc.vector.tensor_scalar(sl, sl, 0.5, None, ALU.mult), reads=[self.mod], writes=[self.mod])
        fw.barrier()
        fw.release(loc)
        st.close()

    def phase_in(self, x, ctx):
        nc, fw = self.nc, self.fw
        st = contextlib.ExitStack()
        loc = []

        def sb(*a, **k):
            b = fw.sb(*a, stack=st, **k)
            loc.append(b)
            return b

        xin = [sb([128, 4, D], F32, "xin") for _ in range(2)]
        xo = [sb([128, 8, 512], F32, "xo") for _ in range(2)]
        pt = [fw.ps([128, 8, 128], F32, "ptr", stack=st) for _ in range(2)]
        loc.extend(pt)
        npt = 0
        def issue_load(ti):
            ng = 4 if ti < 16 else 2
            xi = xin[ti % 2]
            if ti < 16:
                src = x[ti * 512:(ti + 1) * 512, :].rearrange("(g p) d -> p g d", p=128)
            else:
                src = ctx.rearrange("(g p) d -> p g d", p=128)
            fw.dma(fw.sp, xi[:, 0:ng, :], src, xi, True)

        issue_load(0)
        for ti in range(17):
            ng = 4 if ti < 16 else 2
            xi = xin[ti % 2]
            xout = xo[ti % 2]
            if ti + 1 < 17:
                issue_load(ti + 1)
            for g in range(ng):
                p = pt[npt % 2]
                npt += 1
                for c in range(8):
                    fw.op(fw.pe, lambda: nc.tensor.transpose(p[:, c, :], xi[:, g, c * 128:(c + 1) * 128], self.ident[:]),
                          reads=[xi, self.ident], writes=[p])
                eng = fw.dve if g % 2 == 0 else fw.act
                if eng is fw.dve:
                    fw.op(eng, lambda: nc.vector.tensor_copy(xout[:, :, g * 128:(g + 1) * 128], p[:]), reads=[p], writes=[xout])
                else:
                    fw.op(eng, lambda: nc.scalar.copy(xout[:, :, g * 128:(g + 1) * 128], p[:]), reads=[p], writes=[xout])
            w = ng * 128
            fw.dma(fw.sp, self.HTv[:, :, ti * 512:ti * 512 + w], xout[:, :, 0:w], xout, False)
        fw.barrier()
        fw.release(loc)
        st.close()

    def rms_stats(self, hb, w, sq, ps_stat, rstd):
        nc, fw = self.nc, self.fw
        fw.op(fw.act, lambda: nc.scalar.activation(sq[:, :, 0:w], hb[:, :, 0:w], AF.Square), reads=[hb], writes=[sq])
        for c in range(8):
            fw.mm(ps_stat, ps_stat[:, 0:w], self.ones_bf, self.ones_bf[:], sq, sq[:, c, 0:w], c == 0, c == 7)
        fw.op(fw.act, lambda: nc.scalar.activation(rstd[:, 0:w], ps_stat[:, 0:w], AF.Sqrt, bias=self.eps_t[:, 0:1], scale=1.0),
              reads=[ps_stat, self.eps_t], writes=[rstd])
        fw.op(fw.dve, lambda: nc.vector.reciprocal(rstd[:, 0:w], rstd[:, 0:w]), reads=[rstd], writes=[rstd])

    def modulate(self, hb, w, rstd, tmp, outb, li, k_shift, k_scale, m):
        nc, fw = self.nc, self.fw
        for c in range(8):
            t = tmp[c % 2]
            fw.op(fw.dve, lambda: nc.vector.tensor_tensor(t[:, 0:w], hb[:, c, 0:w], rstd[:, 0:w], ALU.mult),
                  reads=[hb, rstd], writes=[t])
            fw.op(fw.act, lambda: nc.scalar.activation(outb[:, c, 0:w], t[:, 0:w], AF.Identity,
                                                       bias=self.modv(li, k_shift, c, m), scale=self.modv(li, k_scale, c, m)),
                  reads=[t, self.mod], writes=[outb])

    def load_cast(self, dst, dst_ap_fn, src_ap_fn, n, stg, shape_w):
        nc, fw = self.nc, self.fw
        for i in range(n):
            s = stg[i % len(stg)]
            fw.dma(fw.sp, s[:, 0:shape_w], src_ap_fn(i), s, True)
            if i % 2 == 0:
                fw.op(fw.dve, lambda: nc.vector.tensor_copy(dst_ap_fn(i), s[:, 0:shape_w]), reads=[s], writes=[dst])
            else:
                fw.op(fw.act, lambda: nc.scalar.copy(dst_ap_fn(i), s[:, 0:shape_w]), reads=[s], writes=[dst])

    def phase_ffn(self, li, which, half, make_u, ctx_live):
        nc, fw = self.nc, self.fw
        st = contextlib.ExitStack()
        loc = []

        def sb(*a, **k):
            b = fw.sb(*a, stack=st, **k)
            loc.append(b)
            return b

        def ps(*a, **k):
            b = fw.ps(*a, stack=st, **k)
            loc.append(b)
            return b

        FH = DFF // 2
        NFC = FH // 128
        f0 = half * FH
        ks = 0 if which == 0 else 6
        wg_b = sb([128, 8, FH], BF16, "wg")
        wu_b = sb([128, 8, FH], BF16, "wu")
        wd_b = sb([128, NFC, D], BF16, "wd")
        stg = [sb([128, FH], F32, "stg") for _ in range(2)]
        wgv = self.wg[li, which].rearrange("(c p) f -> p c f", p=128)
        wuv = self.wu[li, which].rearrange("(c p) f -> p c f", p=128)
        wdv = self.wd[li, which].rearrange("(c p) d -> p c d", p=128)
        self.load_cast(wg_b, lambda i: wg_b[:, i, :], lambda i: wgv[:, i, f0:f0 + FH], 8, stg, FH)
        self.load_cast(wu_b, lambda i: wu_b[:, i, :], lambda i: wuv[:, i, f0:f0 + FH], 8, stg, FH)
        self.load_cast(wd_b, lambda i: wd_b[:, i, :], lambda i: wdv[:, half * NFC + i, :], NFC, stg, D)

        hb2 = [sb([128, 8, 512], F32, "h") for _ in range(2)]
        yb2 = [sb([128, 8, 512], BF16, "y") for _ in range(2 if half == 1 else 1)]
        sq = sb([128, 8, 512], BF16, "sq")
        ab = sb([128, NFC, 512], BF16, "a")
        rstd = sb([128, 512], F32, "rstd")
        tmp = [sb([128, 512], F32, "tmp") for _ in range(2)]
        sg = [sb([128, 512], F32, "sg") for _ in range(2)]
        add_mix = (which == 1 and half == 0)
        ub = sb([128, 8, 512], F32, "u") if (make_u or add_mix) else None
        ps_stat = ps([128, 512], F32, "pstat")
        ps_g = [ps([128, 512], F32, "pg") for _ in range(2)]
        ps_u = [ps([128, 512], F32, "pu") for _ in range(2)]
        ps_d = [ps([128, 512], F32, "pd") for _ in range(2)]

        ntiles = 17 if ctx_live else 16
        import os
        if os.environ.get('DBG_NOU'):
            make_u = False
        if os.environ.get('DBG_NT'):
            ntiles = int(os.environ['DBG_NT'])
        def issue_load(ti):
            w = 512 if ti < 16 else 256
            c0 = ti * 512
            hb = hb2[ti % 2]
            fw.dma(fw.sp, hb[:, :, 0:w], self.HTv[:, :, c0:c0 + w], hb, True)
            if half == 1:
                yb = yb2[ti % 2]
                fw.dma(fw.sp, yb[:, :, 0:w], self.YTv[:, :, c0:c0 + w], yb, True)

        issue_load(0)
        for ti in range(ntiles):
            w = 512 if ti < 16 else 256
            m = 0 if ti < 16 else 1
            c0 = ti * 512
            hb = hb2[ti % 2]
            yb = yb2[ti % len(yb2)]
            if add_mix:
                fw.dma(fw.sp, ub[:, :, 0:w], self.MTv[:, :, c0:c0 + w], ub, True)
            if ti + 1 < ntiles:
                issue_load(ti + 1)
            if add_mix:
                for c in range(8):
                    fw.op(fw.dve, lambda: nc.vector.scalar_tensor_tensor(hb[:, c, 0:w], ub[:, c, 0:w], self.modv(li, 5, c, m),
                                                                         hb[:, c, 0:w], ALU.mult, ALU.add),
                          reads=[ub, hb, self.mod], writes=[hb])
            if half == 0:
                self.rms_stats(hb, w, sq, ps_stat, rstd)
                self.modulate(hb, w, rstd, tmp, yb, li, ks, ks + 1, m)
                fw.dma(fw.sp, self.YTv[:, :, c0:c0 + w], yb[:, :, 0:w], yb, False)
            for fc in range(NFC):
                pg = ps_g[fc % 2]
                pu = ps_u[fc % 2]
                for c in range(8):
                    fw.mm(pg, pg[:, 0:w], wg_b, wg_b[:, c, fc * 128:(fc + 1) * 128], yb, yb[:, c, 0:w], c == 0, c == 7)
                for c in range(8):
                    fw.mm(pu, pu[:, 0:w], wu_b, wu_b[:, c, fc * 128:(fc + 1) * 128], yb, yb[:, c, 0:w], c == 0, c == 7)
                s = sg[fc % 2]
                fw.op(fw.act, lambda: nc.scalar.activation(s[:, 0:w], pg[:, 0:w], AF.Silu), reads=[pg], writes=[s])
                fw.op(fw.dve, lambda: nc.vector.tensor_tensor(ab[:, fc, 0:w], s[:, 0:w], pu[:, 0:w], ALU.mult),
                      reads=[s, pu], writes=[ab])
            for dc in range(8):
                pd = ps_d[dc % 2]
                for fc in range(NFC):
                    fw.mm(pd, pd[:, 0:w], wd_b, wd_b[:, fc, dc * 128:(dc + 1) * 128], ab, ab[:, fc, 0:w],
                          fc == 0, fc == NFC - 1)
                fw.op(fw.dve, lambda: nc.vector.scalar_tensor_tensor(hb[:, dc, 0:w], pd[:, 0:w], self.modv(li, ks + 2, dc, m),
                                                                     hb[:, dc, 0:w], ALU.mult, ALU.add),
                      reads=[pd, hb, self.mod], writes=[hb])
            fw.dma(fw.sp, self.HTv[:, :, c0:c0 + w], hb[:, :, 0:w], hb, False)
            if make_u:
                self.rms_stats(hb, w, sq, ps_stat, rstd)
                self.modulate(hb, w, rstd, tmp, ub, li, 3, 4, m)
                fw.dma(fw.sp, self.UTv[:, :, c0:c0 + w], ub[:, :, 0:w], ub, False)
        fw.barrier()
        fw.release(loc)
        st.close()


    def declare_mixer_inputs(self):
        self.hg_w_in = self.din("hg_w_in", [2, D, 5 * D])
        self.hg_w_out = self.din("hg_w_out", [2, D, D])
        self.hg_lb = self.din("hg_lb_logits", [2, 2, D])
        self.hg_gain = self.din("hg_norm_gain", [2, D])
        self.tri = self.din("tri", [4, 128, 128])
        self.OT = self.scratch("OT", [NT, D])
        self.declare_hyena_inputs()

    def mixer_inputs(self):
        self.declare_mixer_inputs()

    def proj_tm(self, X, ubf, wb, col0, ncols=1024):
        fw = self.fw
        for n in range(ncols // 512):
            for c in range(8):
                fw.mm(X, X[:, n * 512:(n + 1) * 512], ubf, ubf[:, c, :], wb, wb[:, c, col0 + n * 512:col0 + (n + 1) * 512],
                      c == 0, c == 7)

    def load_w_bf(self, dst, src_view, col_map, stg):
        nc, fw = self.nc, self.fw
        i = 0
        for j, sc in enumerate(col_map):
            for c in range(8):
                s_ = stg[i % 2]
                fw.dma(fw.sp, s_[:], src_view[:, c, sc * 1024:(sc + 1) * 1024], s_, True)
                if i % 2 == 0:
                    fw.op(fw.dve, lambda: nc.vector.tensor_copy(dst[:, c, j * 1024:(j + 1) * 1024], s_[:]), reads=[s_], writes=[dst])
                else:
                    fw.op(fw.act, lambda: nc.scalar.copy(dst[:, c, j * 1024:(j + 1) * 1024], s_[:]), reads=[s_], writes=[dst])
                i += 1

    def phase_hgrn(self, li):
        j = li // 3
        ctx_out = li != DEPTH - 1
        self.hgrn_sweep(li, j, 0)
        self.hgrn_sweep(li, j, 1)
        self.hgrn_readout(li, j, ctx_out)

    def hgrn_sweep(self, li, j, dr):
        nc, fw = self.nc, self.fw
        st = contextlib.ExitStack()
        loc = []

        def sb(*a, **k):
            b = fw.sb(*a, stack=st, **k)
            loc.append(b)
            return b

        wb = sb([128, 8, 3072], BF16, "hgw")
        stg = [sb([128, 1024], F32, "hgstg") for _ in range(2)]
        wv = self.hg_w_in[j].rearrange("(c p) f -> p c f", p=128)
        self.load_w_bf(wb, wv, [0, 1 + dr, 3], stg)
        tri = sb([128, 128], F32, "tri")
        bones = sb([128, 128], F32, "bones")
        ind = sb([128, 128], F32, "ind")
        M8 = sb([128, 8, 128], F32, "M8")
        fw.dma(fw.sp, tri[:], self.tri[dr], tri, True)
        fw.dma(fw.sp, bones[:], self.tri[2], bones, True)
        fw.dma(fw.sp, ind[:], self.tri[3], ind, True)
        for h in range(8):
            fw.op(fw.dve, lambda: nc.vector.tensor_copy(M8[:, h, :], tri[:]), reads=[tri], writes=[M8])
        lbb = sb([128, 1024], F32, "lbb")
        oml = sb([128, 1024], F32, "oml")
        if j == 0:
            fw.op(fw.dve, lambda: nc.vector.memset(lbb[:], 0.0), writes=[lbb])
            fw.op(fw.dve, lambda: nc.vector.memset(oml[:], 1.0), writes=[oml])
        else:
            fw.dma(fw.sp, lbb[:], self.hg_lb[dr, 1:2, :].to_broadcast([128, D]), lbb, True)
            fw.dma(fw.sp, oml[:], self.hg_lb[dr, 0:1, :].to_broadcast([128, D]), oml, True)
            fw.op(fw.dve, lambda: nc.vector.tensor_tensor(lbb[:], lbb[:], oml[:], ALU.subtract), reads=[lbb, oml], writes=[lbb])
            fw.op(fw.act, lambda: nc.scalar.activation(lbb[:], lbb[:], AF.Sigmoid), reads=[lbb], writes=[lbb])
            fw.op(fw.dve, lambda: nc.vector.tensor_scalar(oml[:], lbb[:], -1.0, 1.0, ALU.mult, ALU.add), reads=[lbb], writes=[oml])

        uf = [sb([128, 8, 128], F32, "uf") for _ in range(2)]
        ubf = sb([128, 8, 128], BF16, "ubf")
        qs = sb([128, 1024], F32, "qs")
        fk = sb([128, 1024], F32, "fk")
        lf = sb([128, 1024], F32, "lf")
        vb = sb([128, 1024], BF16, "vb")
        bc = sb([128, 1024], F32, "bc")
        e1 = sb([128, 1024], F32, "e1")
        kt = sb([128, 1024], F32, "kt")
        Kc = [sb([128, 1024], BF16, "Kc") for _ in range(4)]
        Qc = [sb([128, 8, 128], BF16, "Qc") for _ in range(4)]
        qTb = sb([128, 8, 128], BF16, "qTb")
        kTb = sb([128, 8, 128], BF16, "kTb")
        PT = sb([128, 8, 128], BF16, "PT")
        S = sb([128, 1024], F32, "S")
        Sb = sb([128, 1024], BF16, "Sb")
        ebl = sb([128, 8, 4], F32, "ebl")
        ob = [sb([128, 1024], F32, "ob") for _ in range(2)]
        of = [sb([128, 1024], F32, "of") for _ in range(2)] if dr == 1 else None
        X = [fw.ps([128, 1024], F32, "X", stack=st) for _ in range(4)]
        loc.extend(X)
        for q in Qc:
            fw.op(fw.dve, lambda: nc.vector.memset(q[:], 0.0), writes=[q])
        fw.op(fw.dve, lambda: nc.vector.memset(S[:], 0.0), writes=[S])
        fw.op(fw.dve, lambda: nc.vector.memset(Sb[:], 0.0), writes=[Sb])

        if dr == 0:
            groups = [L + 0, L + 128] + [g * 128 for g in range(64)]
            corder = [0, 1, 2, 3]
        else:
            groups = [L + 128, L + 0] + [g * 128 for g in range(63, -1, -1)]
            corder = [3, 2, 1, 0]
        import os
        if os.environ.get("DBG_NG"):
            ng_ = int(os.environ["DBG_NG"])
            groups = [g_ for g_ in groups if g_ >= L or g_ < 128 * ng_]

        def issue_load(gi):
            t0 = groups[gi]
            fw.dma(fw.sp, uf[gi % 2][:], self.UTv[:, :, t0:t0 + 128], uf[gi % 2], True)
            if dr == 1:
                fw.dma(fw.sp, of[gi % 2][:], self.OT[t0:t0 + 128, :], of[gi % 2], True)

        vb2 = [vb, sb([128, 1024], BF16, "vb2")]

        def front_pieces(gi):
            u = uf[gi % 2]
            vbn = vb2[gi % 2]

            def pA():
                fw.op(fw.dve, lambda: nc.vector.tensor_copy(ubf[:], u[:]), reads=[u], writes=[ubf])
                self.proj_tm(X[3], ubf, wb, 0)
                fw.op(fw.act, lambda: nc.scalar.activation(qs[:], X[3][:], AF.Silu), reads=[X[3]], writes=[qs])

            def pB():
                self.proj_tm(X[3], ubf, wb, 1024)
                fw.op(fw.act, lambda: nc.scalar.activation(fk[:], X[3][:], AF.Sigmoid), reads=[X[3]], writes=[fk])

            def pC():
                self.proj_tm(X[3], ubf, wb, 2048)
                fw.op(fw.act, lambda: nc.scalar.copy(vbn[:], X[3][:]), reads=[X[3]], writes=[vbn])

            def pD():
                fw.op(fw.dve, lambda: nc.vector.tensor_tensor(fk[:], fk[:], oml[:], ALU.mult), reads=[fk, oml], writes=[fk])
                fw.op(fw.dve, lambda: nc.vector.tensor_tensor(fk[:], fk[:], lbb[:], ALU.add), reads=[fk, lbb], writes=[fk])
                fw.op(fw.act, lambda: nc.scalar.activation(lf[:], fk[:], AF.Ln), reads=[fk], writes=[lf])
                fw.op(fw.dve, lambda: nc.vector.tensor_scalar(fk[:], fk[:], -1.0, 1.0, ALU.mult, ALU.add), reads=[fk], writes=[fk])
            return [pA, pB, pC, pD]

        issue_load(0)
        for p_ in front_pieces(0):
            p_()
        for gi, t0 in enumerate(groups):
            if gi + 1 < len(groups):
                issue_load(gi + 1)
                nxt = front_pieces(gi + 1)
            else:
                nxt = [None] * 4
            vb = vb2[gi % 2]
            for h in range(8):
                fw.mm(X[3], X[3][:, h * 4:h * 4 + 4], lf, lf[:, h * 128:(h + 1) * 128], ind, ind[:, 0:4], True, True)
            fw.op(fw.act, lambda: nc.scalar.activation(ebl[:], X[3][:, 0:32].rearrange("p (h c) -> p h c", c=4), AF.Exp),
                  reads=[X[3]], writes=[ebl])
            for n in range(2):
                fw.mm(X[3], X[3][:, n * 512:(n + 1) * 512], tri, tri[:], lf, lf[:, n * 512:(n + 1) * 512], True, True)
            for n in range(2):
                fw.mm(X[0], X[0][:, n * 512:(n + 1) * 512], bones, bones[:], lf, lf[:, n * 512:(n + 1) * 512], True, True)
            fw.op(fw.act, lambda: nc.scalar.copy(bc[:], X[3][:]), reads=[X[3]], writes=[bc])
            fw.op(fw.act, lambda: nc.scalar.activation(e1[:], bc[:], AF.Exp), reads=[bc], writes=[e1])
            fw.op(fw.dve, lambda: nc.vector.tensor_tensor(qs[:], qs[:], e1[:], ALU.mult), reads=[qs, e1], writes=[qs])
            fw.op(fw.act, lambda: nc.scalar.activation(e1[:], bc[:], AF.Exp, scale=-1.0), reads=[bc], writes=[e1])
            fw.op(fw.dve, lambda: nc.vector.tensor_tensor(kt[:], fk[:], e1[:], ALU.mult), reads=[fk, e1], writes=[kt])
            fw.op(fw.dve, lambda: nc.vector.tensor_tensor(bc[:], X[0][:], bc[:], ALU.subtract), reads=[X[0], bc], writes=[bc])
            fw.op(fw.act, lambda: nc.scalar.activation(e1[:], bc[:], AF.Exp), reads=[bc], writes=[e1])
            for c in range(4):
                fw.op(fw.dve, lambda: nc.vector.scalar_tensor_tensor(Kc[c][:], fk[:], ind[:, c:c + 1], e1[:], ALU.mult, ALU.mult),
                      reads=[fk, ind, e1], writes=[Kc[c]])
            for h in range(8):
                fw.op(fw.pe, lambda: nc.tensor.transpose(X[1][:, h * 128:(h + 1) * 128], qs[:, h * 128:(h + 1) * 128], self.ident[:]),
                      reads=[qs, self.ident], writes=[X[1]])
            x1v = X[1][:].rearrange("p (h t) -> p h t", t=128)
            fw.op(fw.act, lambda: nc.scalar.copy(qTb[:], x1v), reads=[X[1]], writes=[qTb])
            for c in range(4):
                fw.op(fw.dve, lambda: nc.vector.tensor_copy(Qc[c][:, :, 32 * c:32 * c + 32], x1v[:, :, 32 * c:32 * c + 32]),
                      reads=[X[1]], writes=[Qc[c]])
            for h in range(8):
                fw.op(fw.pe, lambda: nc.tensor.transpose(X[2][:, h * 128:(h + 1) * 128], kt[:, h * 128:(h + 1) * 128], self.ident[:]),
                      reads=[kt, self.ident], writes=[X[2]])
            fw.op(fw.act, lambda: nc.scalar.copy(kTb[:], X[2][:].rearrange("p (h t) -> p h t", t=128)), reads=[X[2]], writes=[kTb])
            for h in range(8):
                fw.mm(X[3], X[3][:, h * 128:(h + 1) * 128], kTb, kTb[:, h, :], qTb, qTb[:, h, :], True, True)
            fw.op(fw.dve, lambda: nc.vector.tensor_tensor(PT[:], X[3][:].rearrange("p (h t) -> p h t", t=128), M8[:], ALU.mult),
                  reads=[X[3], M8], writes=[PT])
            for h in range(8):
                fw.mm(X[0], X[0][:, h * 128:(h + 1) * 128], PT, PT[:, h, :], vb, vb[:, h * 128:(h + 1) * 128],
                      h % 4 == 0, False, skip_group_check=True)
            for ci, c in enumerate(corder):
                for h in range(8):
                    fw.mm(X[0], X[0][:, h * 128:(h + 1) * 128], Qc[c], Qc[c][:, h, :], Sb, Sb[:, h * 128:(h + 1) * 128],
                          False, ci == 3, skip_group_check=True)
                Xd = X[1 + (ci % 2)]
                for h in range(8):
                    fw.mm(Xd, Xd[:, h * 128:(h + 1) * 128], Kc[c], Kc[c][:, h * 128:(h + 1) * 128], vb, vb[:, h * 128:(h + 1) * 128],
                          True, True)
                for h in range(8):
                    fw.op(fw.dve, lambda: nc.vector.scalar_tensor_tensor(S[:, h * 128:(h + 1) * 128], S[:, h * 128:(h + 1) * 128],
                                                                         ebl[:, h, c:c + 1], Xd[:, h * 128:(h + 1) * 128], ALU.mult, ALU.add),
                          reads=[S, ebl, Xd], writes=[S])
                fw.op(fw.act, lambda: nc.scalar.copy(Sb[:], S[:]), reads=[S], writes=[Sb])
                if nxt[ci] is not None:
                    nxt[ci]()
            o = ob[gi % 2]
            if dr == 0:
                fw.op(fw.act, lambda: nc.scalar.copy(o[:], X[0][:]), reads=[X[0]], writes=[o])
            else:
                fw.op(fw.dve, lambda: nc.vector.tensor_tensor(o[:], X[0][:], of[gi % 2][:], ALU.add), reads=[X[0], of[gi % 2]], writes=[o])
            fw.dma(fw.sp, self.OT[t0:t0 + 128, :], o[:], o, False)
        fw.barrier()
        fw.release(loc)
        st.close()

    def hgrn_readout(self, li, j, ctx_out):
        nc, fw = self.nc, self.fw
        st = contextlib.ExitStack()
        loc = []

        def sb(*a, **k):
            b = fw.sb(*a, stack=st, **k)
            loc.append(b)
            return b

        wg = sb([128, 8, 1024], BF16, "hgwg")
        wo = sb([128, 8, 1024], BF16, "hgwo")
        stg = [sb([128, 1024], F32, "hgstg") for _ in range(2)]
        self.load_w_bf(wg, self.hg_w_in[j].rearrange("(c p) f -> p c f", p=128), [4], stg)
        self.load_w_bf(wo, self.hg_w_out[j].rearrange("(c p) f -> p c f", p=128), [0], stg)
        gain = sb([128, 1024], F32, "gain")
        fw.dma(fw.sp, gain[:], self.hg_gain[j:j + 1, :].to_broadcast([128, D]), gain, True)
        uf = [sb([128, 8, 128], F32, "uf") for _ in range(2)]
        ot = [sb([128, 1024], F32, "ot") for _ in range(2)]
        ubf = sb([128, 8, 128], BF16, "ubf")
        sgt = sb([128, 1024], F32, "sgt")
        sq = sb([128, 1024], F32, "sq")
        ss = sb([128, 8], F32, "ss")
        r = sb([128, 1024], F32, "r")
        rT = sb([128, 8, 128], BF16, "rT")
        yo = [sb([128, 8, 128], F32, "yo") for _ in range(2)]
        X = [fw.ps([128, 1024], F32, "X", stack=st) for _ in range(3)]
        loc.extend(X)
        groups = ([L, L + 128] if ctx_out else []) + [g * 128 for g in range(64)]
        import os
        if os.environ.get("DBG_NG"):
            groups = groups[:int(os.environ["DBG_NG"])]

        def issue_load(gi):
            t0 = groups[gi]
            fw.dma(fw.sp, uf[gi % 2][:], self.UTv[:, :, t0:t0 + 128], uf[gi % 2], True)
            fw.dma(fw.sp, ot[gi % 2][:], self.OT[t0:t0 + 128, :], ot[gi % 2], True)

        issue_load(0)
        for gi, t0 in enumerate(groups):
            if gi + 1 < len(groups):
                issue_load(gi + 1)
            u = uf[gi % 2]
            o = ot[gi % 2]
            fw.op(fw.dve, lambda: nc.vector.tensor_copy(ubf[:], u[:]), reads=[u], writes=[ubf])
            self.proj_tm(X[0], ubf, wg, 0)
            fw.op(fw.act, lambda: nc.scalar.activation(sgt[:], X[0][:], AF.Silu), reads=[X[0]], writes=[sgt])
            fw.op(fw.dve, lambda: nc.vector.tensor_tensor(sq[:], o[:], o[:], ALU.mult), reads=[o], writes=[sq])
            fw.op(fw.dve, lambda: nc.vector.reduce_sum(ss[:], sq[:].rearrange("p (h e) -> p h e", e=128), AX.X), reads=[sq], writes=[ss])
            fw.op(fw.act, lambda: nc.scalar.activation(ss[:], ss[:], AF.Sqrt, bias=self.eps_t[:, 0:1], scale=1.0 / 128), reads=[ss, self.eps_t], writes=[ss])
            fw.op(fw.dve, lambda: nc.vector.reciprocal(ss[:], ss[:]), reads=[ss], writes=[ss])
            for h in range(8):
                fw.op(fw.act, lambda: nc.scalar.activation(r[:, h * 128:(h + 1) * 128], o[:, h * 128:(h + 1) * 128], AF.Copy, scale=ss[:, h:h + 1]),
                      reads=[o, ss], writes=[r])
            fw.op(fw.dve, lambda: nc.vector.tensor_tensor(r[:], r[:], gain[:], ALU.mult), reads=[r, gain], writes=[r])
            fw.op(fw.dve, lambda: nc.vector.tensor_tensor(r[:], r[:], sgt[:], ALU.mult), reads=[r, sgt], writes=[r])
            for h in range(8):
                fw.op(fw.pe, lambda: nc.tensor.transpose(X[1][:, h * 128:(h + 1) * 128], r[:, h * 128:(h + 1) * 128], self.ident[:]),
                      reads=[r, self.ident], writes=[X[1]])
            fw.op(fw.act, lambda: nc.scalar.copy(rT[:], X[1][:].rearrange("p (h t) -> p h t", t=128)), reads=[X[1]], writes=[rT])
            for dc in range(8):
                for jc in range(8):
                    fw.mm(X[2], X[2][:, dc * 128:(dc + 1) * 128], wo, wo[:, jc, dc * 128:(dc + 1) * 128], rT, rT[:, jc, :], jc == 0, jc == 7)
            y = yo[gi % 2]
            fw.op(fw.dve, lambda: nc.vector.tensor_copy(y[:], X[2][:].rearrange("p (c t) -> p c t", t=128)), reads=[X[2]], writes=[y])
            fw.dma(fw.sp, self.MTv[:, :, t0:t0 + 128], y[:], y, False)
        fw.barrier()
        fw.release(loc)
        st.close()


    def declare_hyena_inputs(self):
        self.hy_w_in = self.din("hy_w_in", [D, 3 * D])
        self.hy_w_out = self.din("hy_w_out", [D, D])
        self.hy_w1 = self.din("hy_w1", [33, 64])
        self.hy_w2 = self.din("hy_w2", [64, 64])
        self.hy_w3 = self.din("hy_w3", [64, 64])
        self.hy_w4 = self.din("hy_w4", [64, 2 * D])
        self.hy_sfb = self.din("hy_sfb", [64, 4])
        self.hy_feats = self.din("hy_feats", [128, NT])
        self.hy_tpos = self.din("hy_tpos", [1, NT])
        self.fftc_d = self.din("fftc", [128, 1920])
        self.ZT = self.scratch("ZT", [3 * D, NT])
        self.VT = self.scratch("VT", [D, NT])
        self.X0T = self.scratch("X0T", [D, NT])
        self.KT = self.scratch("KT", [2 * D, NT])
        self.KFT = self.scratch("KFT", [2, 128, D * 128])
        self.CT = self.scratch("CT", [D, NT])

    def phase_hyena(self, li):
        self.hy_proj()
        self.hy_shortconv()
        self.hy_filter()
        self.hy_fft(True)
        self.hy_fft(False)
        self.hy_ctxconv()
        self.hy_out()

    def _ctx(self):
        st = contextlib.ExitStack()
        loc = []
        fw = self.fw

        def sb(*a, **k):
            b = fw.sb(*a, stack=st, **k)
            loc.append(b)
            return b

        def ps(*a, **k):
            b = fw.ps(*a, stack=st, **k)
            loc.append(b)
            return b

        def done():
            fw.barrier()
            fw.release(loc)
            st.close()
        return sb, ps, done

    def hy_proj(self):
        nc, fw = self.nc, self.fw
        sb, ps, done = self._ctx()
        wb = sb([128, 8, 3072], BF16, "hyw")
        stg = [sb([128, 1024], F32, "hystg") for _ in range(2)]
        self.load_w_bf(wb, self.hy_w_in.rearrange("(c p) f -> p c f", p=128), [0, 1, 2], stg)
        uf = [sb([128, 8, 512], F32, "uf") for _ in range(2)]
        ubf = sb([128, 8, 512], BF16, "ubf")
        zo = [sb([128, 8, 512], F32, "zo") for _ in range(2)]
        pp = [ps([128, 512], F32, "pp") for _ in range(4)]
        ZTv = self.ZT.rearrange("(c p) t -> p c t", p=128)

        def issue_load(ti):
            w = 512 if ti < 16 else 256
            fw.dma(fw.sp, uf[ti % 2][:, :, 0:w], self.UTv[:, :, ti * 512:ti * 512 + w], uf[ti % 2], True)

        issue_load(0)
        n = 0
        for ti in range(17):
            w = 512 if ti < 16 else 256
            if ti + 1 < 17:
                issue_load(ti + 1)
            u = uf[ti % 2]
            fw.op(fw.dve, lambda: nc.vector.tensor_copy(ubf[:, :, 0:w], u[:, :, 0:w]), reads=[u], writes=[ubf])
            for k3 in range(3):
                z = zo[(ti * 3 + k3) % 2]
                for fc8 in range(8):
                    fc = k3 * 8 + fc8
                    p = pp[n % 4]
                    n += 1
                    for c in range(8):
                        fw.mm(p, p[:, 0:w], wb, wb[:, c, fc * 128:(fc + 1) * 128], ubf, ubf[:, c, 0:w], c == 0, c == 7)
                    fw.op(fw.act, lambda: nc.scalar.activation(z[:, fc8, 0:w], p[:, 0:w], AF.Identity, bias=self.vec("hy_b_in", fc), scale=1.0),
                          reads=[p, self.vecs], writes=[z])
                fw.dma(fw.sp, ZTv[:, k3 * 8:(k3 + 1) * 8, ti * 512:ti * 512 + w], z[:, :, 0:w], z, False)
        done()

    def hy_shortconv(self):
        nc, fw = self.nc, self.fw
        sb, ps, done = self._ctx()
        zr = [sb([128, NT], F32, "zr") for _ in range(2)]
        o = [sb([128, NT], F32, "o") for _ in range(2)]
        ZTv = self.ZT.rearrange("(c p) t -> p c t", p=128)
        VTv = self.VT.rearrange("(c p) t -> p c t", p=128)
        X0v = self.X0T.rearrange("(c p) t -> p c t", p=128)
        order = []
        for c in range(8):
            order += [(8 + c, 0), (16 + c, 1), (c, 0)]

        def issue_load(i):
            fw.dma(fw.sp, zr[i % 2][:], ZTv[:, order[i][0], :], zr[i % 2], True)

        issue_load(0)
        for i, (fc, oi) in enumerate(order):
            if i + 1 < len(order):
                issue_load(i + 1)
            z = zr[i % 2]
            ob = o[oi]
            w0, w1, w2, cb = (self.vec("hy_conv_w0", fc), self.vec("hy_conv_w1", fc), self.vec("hy_conv_w2", fc), self.vec("hy_conv_b", fc))
            fw.op(fw.act, lambda: nc.scalar.activation(ob[:], z[:], AF.Identity, bias=cb, scale=w1), reads=[z, self.vecs], writes=[ob])
            for (a, b_) in ((0, L), (L, NT)):
                fw.op(fw.dve, lambda: nc.vector.scalar_tensor_tensor(ob[:, a + 1:b_], z[:, a:b_ - 1], w0, ob[:, a + 1:b_], ALU.mult, ALU.add),
                      reads=[z, ob, self.vecs], writes=[ob])
                fw.op(fw.dve, lambda: nc.vector.scalar_tensor_tensor(ob[:, a:b_ - 1], z[:, a + 1:b_], w2, ob[:, a:b_ - 1], ALU.mult, ALU.add),
                      reads=[z, ob, self.vecs], writes=[ob])
            c = fc % 8
            if fc >= 16:
                fw.op(fw.dve, lambda: nc.vector.tensor_tensor(o[1][:], o[1][:], o[0][:], ALU.mult), reads=[o[0], o[1]], writes=[o[1]])
                fw.dma(fw.sp, VTv[:, c, :], o[1][:], o[1], False)
            elif fc < 8:
                fw.dma(fw.sp, X0v[:, c, :], o[0][:], o[0], False)
        done()

    def hy_filter(self):
        nc, fw = self.nc, self.fw
        sb, ps, done = self._ctx()
        w1 = sb([128, 128], F32, "w1")
        w2 = sb([128, 128], F32, "w2")
        w3 = sb([128, 128], F32, "w3")
        w4 = sb([128, 2 * D], F32, "w4")
        sfb = sb([128, 4], F32, "sfb")
        sfbb = sb([128, 3], F32, "sfbb")
        for t_ in (w1, w2, w3, w4, sfb):
            fw.op(fw.dve, lambda: nc.vector.memset(t_[:], 0.0), writes=[t_])
        fw.dma(fw.sp, w1[0:33, 0:64], self.hy_w1, w1, True)
        fw.dma(fw.sp, w2[0:64, 0:64], self.hy_w2, w2, True)
        fw.dma(fw.sp, w3[0:64, 0:64], self.hy_w3, w3, True)
        fw.dma(fw.sp, w4[0:64, :], self.hy_w4, w4, True)
        fw.dma(fw.sp, sfb[0:64, :], self.hy_sfb, sfb, True)
        fw.op(fw.dve, lambda: nc.vector.tensor_scalar(sfbb[:], sfb[:, 1:4], sfb[:, 0:1], None, ALU.mult), reads=[sfb], writes=[sfbb])
        ft = [sb([128, 512], F32, "ft") for _ in range(2)]
        tp = [sb([128, 512], F32, "tp") for _ in range(2)]
        hcur = [sb([128, 512], F32, "hc") for _ in range(2)]
        wr = sb([128, 512], F32, "wr")
        dec = [sb([128, 512], F32, "dec") for _ in range(2)]
        ko = [sb([128, 16, 512], F32, "ko") for _ in range(1)]
        pm = [ps([128, 512], F32, "pm") for _ in range(2)]
        pk = [ps([128, 512], F32, "pk") for _ in range(2)]
        KTv = self.KT.rearrange("(c p) t -> p c t", p=128)
        ws = [w1, w2, w3]

        def issue_load(ti):
            w = 512 if ti < 16 else 256
            fw.dma(fw.sp, ft[ti % 2][:, 0:w], self.hy_feats[:, ti * 512:ti * 512 + w], ft[ti % 2], True)
            fw.dma(fw.sp, tp[ti % 2][:, 0:w], self.hy_tpos[0:1, ti * 512:ti * 512 + w].to_broadcast([128, w]), tp[ti % 2], True)

        issue_load(0)
        for ti in range(17):
            w = 512 if ti < 16 else 256
            if ti + 1 < 17:
                issue_load(ti + 1)
            src_b, src_ap = ft[ti % 2], ft[ti % 2][:, 0:w]
            for l in range(3):
                p = pm[l % 2]
                kdim = 33 if l == 0 else 64
                fw.mm(p, p[:, 0:w], ws[l], ws[l][:], src_b, src_ap, True, True)
                h = hcur[l % 2]
                fw.op(fw.dve, lambda: nc.vector.tensor_scalar(h[:, 0:w], p[:, 0:w], sfb[:, 0:1], sfbb[:, l:l + 1], ALU.mult, ALU.add),
                      reads=[p, sfb, sfbb], writes=[h])
                fw.op(fw.dve, lambda: nc.vector.tensor_scalar(wr[:, 0:w], h[:, 0:w], PI, -2 * PI, ALU.is_gt, ALU.mult), reads=[h], writes=[wr])
                fw.op(fw.dve, lambda: nc.vector.tensor_tensor(h[:, 0:w], h[:, 0:w], wr[:, 0:w], ALU.add), reads=[h, wr], writes=[h])
                fw.op(fw.dve, lambda: nc.vector.tensor_scalar(wr[:, 0:w], h[:, 0:w], -PI, 2 * PI, ALU.is_lt, ALU.mult), reads=[h], writes=[wr])
                fw.op(fw.dve, lambda: nc.vector.tensor_tensor(h[:, 0:w], h[:, 0:w], wr[:, 0:w], ALU.add), reads=[h, wr], writes=[h])
                fw.op(fw.act, lambda: nc.scalar.activation(h[:, 0:w], h[:, 0:w], AF.Sin), reads=[h], writes=[h])
                src_b, src_ap = h, h[:, 0:w]
            k_ = ko[0]
            t_ = tp[ti % 2]
            for dd in range(16):
                p = pk[dd % 2]
                fw.mm(p, p[:, 0:w], w4, w4[:, dd * 128:(dd + 1) * 128], src_b, src_ap, True, True)
                de = dec[dd % 2]
                fw.op(fw.act, lambda: nc.scalar.activation(de[:, 0:w], t_[:, 0:w], AF.Exp, scale=self.vec("hy_negdelta", dd % 8)),
                      reads=[t_, self.vecs], writes=[de])
                fw.op(fw.dve, lambda: nc.vector.tensor_tensor(k_[:, dd, 0:w], p[:, 0:w], de[:, 0:w], ALU.mult), reads=[p, de], writes=[k_])
            if ti == 0 or ti == 16:
                fw.op(fw.dve, lambda: nc.vector.memset(k_[:, 8:16, 0:1], 0.0), writes=[k_])
            fw.dma(fw.sp, KTv[:, :, ti * 512:ti * 512 + w], k_[:, :, 0:w], k_, False)
        done()

    def fft_fwd(self, Xin, c0, fc, fcr, Y, Z, t1, t2, Ypr, Ypi):
        nc, fw = self.nc, self.fw
        for ci in range(4):
            fw.mm(Y, Y[:, ci, :], Xin, Xin[:, c0 + ci, :], fcr, fcr[:, 384:640], True, True)
        Yr = Y[:, :, 0:128]
        Yi = Y[:, :, 128:256]
        Tc = fc[:, 896:1408].rearrange("p (c k) -> p c k", k=128)
        Ts = fc[:, 1408:1920].rearrange("p (c k) -> p c k", k=128)
        fw.op(fw.dve, lambda: nc.vector.tensor_tensor(t1[:], Yr, Tc, ALU.mult), reads=[Y, fc], writes=[t1])
        fw.op(fw.dve, lambda: nc.vector.tensor_tensor(t2[:], Yi, Ts, ALU.mult), reads=[Y, fc], writes=[t2])
        fw.op(fw.dve, lambda: nc.vector.tensor_tensor(Ypr[:], t1[:], t2[:], ALU.add), reads=[t1, t2], writes=[Ypr])
        fw.op(fw.dve, lambda: nc.vector.tensor_tensor(t1[:], Yi, Tc, ALU.mult), reads=[Y, fc], writes=[t1])
        fw.op(fw.dve, lambda: nc.vector.tensor_tensor(t2[:], Yr, Ts, ALU.mult), reads=[Y, fc], writes=[t2])
        fw.op(fw.dve, lambda: nc.vector.tensor_tensor(Ypi[:], t1[:], t2[:], ALU.subtract), reads=[t1, t2], writes=[Ypi])
        C, S_, nS = fcr[:, 128:256], fcr[:, 256:384], fcr[:, 0:128]
        yr = Ypr[:].rearrange("p c k -> p (c k)")
        yi = Ypi[:].rearrange("p c k -> p (c k)")
        fw.mm(Z[0], Z[0][:], fcr, C, Ypr, yr, True, False)
        fw.mm(Z[0], Z[0][:], fcr, S_, Ypi, yi, False, True)
        fw.mm(Z[1], Z[1][:], fcr, C, Ypi, yi, True, False)
        fw.mm(Z[1], Z[1][:], fcr, nS, Ypr, yr, False, True)

    def hy_fft(self, is_filter):
        nc, fw = self.nc, self.fw
        sb, ps, done = self._ctx()
        fc = sb([128, 1920], F32, "fftc")
        fw.dma(fw.sp, fc[:], self.fftc_d, fc, True)
        fcr = sb([128, 896], F32R, "fftcr")
        fw.op(fw.act, lambda: nc.scalar.copy(fcr[:], fc[:, 0:896]), reads=[fc], writes=[fcr])
        xr = [sb([128, 16, 128], F32R, "xr") for _ in range(2)]
        NSB = 16
        xin = [sb([128, NSB, 128], F32, "xin") for _ in range(2)]
        xin2 = [sb([128, NSB, 128], F32, "xin2") for _ in range(2)] if is_filter else None
        for t_ in xin + (xin2 or []):
            fw.op(fw.dve, lambda: nc.vector.memset(t_[:], 0.0), writes=[t_])
        kfr = [sb([128, NSB * 128], F32, "kfr") for _ in range(2)]
        kfi = [sb([128, NSB * 128], F32, "kfi") for _ in range(2)]
        xo = None if is_filter else [sb([64, NSB, 128], F32, "xo") for _ in range(2)]
        t1 = sb([128, 4, 128], F32, "t1")
        t2 = sb([128, 4, 128], F32, "t2")
        Ypr = sb([128, 4, 128], F32R, "Ypr")
        Ypi = sb([128, 4, 128], F32R, "Ypi")
        Ar = sb([128, 512], F32, "Ar")
        Ai = sb([128, 512], F32, "Ai")
        Wr = sb([128, 4, 128], F32R, "Wr")
        Wi = sb([128, 4, 128], F32R, "Wi")
        Y = ps([128, 4, 256], F32, "Y")
        Z = [ps([128, 512], F32, "Z") for _ in range(2)]
        if not is_filter:
            W = ps([128, 4, 256], F32, "W")
            XO = ps([128, 512], F32, "XO")
        KFv = self.KFT
        nsb = D // NSB
        import os
        if os.environ.get("DBG_NSB"):
            nsb = int(os.environ["DBG_NSB"])

        def issue_load(si):
            ch0 = si * NSB
            if is_filter:
                fw.dma(fw.sp, xin[si % 2][0:64], self.KT[ch0:ch0 + NSB, 0:L].rearrange("c (a b) -> a c b", b=128)[0:64], xin[si % 2], True)
                fw.dma(fw.sp, xin2[si % 2][0:64], self.KT[D + ch0:D + ch0 + NSB, 0:L].rearrange("c (a b) -> a c b", b=128)[0:64], xin2[si % 2], True)
            else:
                fw.dma(fw.sp, xin[si % 2][0:64], self.VT[ch0:ch0 + NSB, 0:L].rearrange("c (a b) -> a c b", b=128)[0:64], xin[si % 2], True)
                fw.dma(fw.sp, kfr[si % 2][:], KFv[0, :, ch0 * 128:(ch0 + NSB) * 128], kfr[si % 2], True)
                fw.dma(fw.sp, kfi[si % 2][:], KFv[1, :, ch0 * 128:(ch0 + NSB) * 128], kfi[si % 2], True)

        issue_load(0)
        for si in range(nsb):
            if si + 1 < nsb:
                issue_load(si + 1)
            ch0 = si * NSB
            X1 = xr[0]
            fw.op(fw.act, lambda: nc.scalar.copy(X1[:], xin[si % 2][:]), reads=[xin[si % 2]], writes=[X1])
            if is_filter:
                X2 = xr[1]
                fw.op(fw.act, lambda: nc.scalar.copy(X2[:], xin2[si % 2][:]), reads=[xin2[si % 2]], writes=[X2])
            kr, ki = kfr[si % 2], kfi[si % 2]
            for bi in range(NSB // 4):
                c0 = bi * 4
                sl = slice(c0 * 128, (c0 + 4) * 128)
                self.fft_fwd(X1, c0, fc, fcr, Y, Z, t1, t2, Ypr, Ypi)
                if is_filter:
                    fw.op(fw.act, lambda: nc.scalar.copy(Ar[:], Z[0][:]), reads=[Z[0]], writes=[Ar])
                    fw.op(fw.act, lambda: nc.scalar.copy(Ai[:], Z[1][:]), reads=[Z[1]], writes=[Ai])
                    self.fft_fwd(X2, c0, fc, fcr, Y, Z, t1, t2, Ypr, Ypi)
                    fw.op(fw.dve, lambda: nc.vector.tensor_tensor(kr[:, sl], Z[0][:], Ar[:], ALU.add), reads=[Z[0], Ar], writes=[kr])
                    fw.op(fw.dve, lambda: nc.vector.tensor_tensor(ki[:, sl], Ai[:], Z[1][:], ALU.subtract), reads=[Z[1], Ai], writes=[ki])
                    continue
                Pr = Ypr[:].rearrange("p c k -> p (c k)")
                Pi = Ypi[:].rearrange("p c k -> p (c k)")
                a1 = t1[:].rearrange("p c k -> p (c k)")
                a2 = t2[:].rearrange("p c k -> p (c k)")
                fw.op(fw.dve, lambda: nc.vector.tensor_tensor(a1, Z[0][:], kr[:, sl], ALU.mult), reads=[Z[0], kr], writes=[t1])
                fw.op(fw.dve, lambda: nc.vector.tensor_tensor(a2, Z[1][:], ki[:, sl], ALU.mult), reads=[Z[1], ki], writes=[t2])
                fw.op(fw.dve, lambda: nc.vector.tensor_tensor(Pr, a1, a2, ALU.subtract), reads=[t1, t2], writes=[Ypr])
                fw.op(fw.dve, lambda: nc.vector.tensor_tensor(a1, Z[0][:], ki[:, sl], ALU.mult), reads=[Z[0], ki], writes=[t1])
                fw.op(fw.dve, lambda: nc.vector.tensor_tensor(a2, Z[1][:], kr[:, sl], ALU.mult), reads=[Z[1], kr], writes=[t2])
                fw.op(fw.dve, lambda: nc.vector.tensor_tensor(Pi, a1, a2, ALU.add), reads=[t1, t2], writes=[Ypi])
                for ci in range(4):
                    fw.mm(W, W[:, ci, :], Ypr, Ypr[:, ci, :], fcr, fcr[:, 128:384], True, False)
                    fw.mm(W, W[:, ci, :], Ypi, Ypi[:, ci, :], fcr, fcr[:, 0:256], False, True)
                Wre = W[:, :, 0:128]
                Wim = W[:, :, 128:256]
                Tc = fc[:, 896:1408].rearrange("p (c k) -> p c k", k=128)
                Ts = fc[:, 1408:1920].rearrange("p (c k) -> p c k", k=128)
                fw.op(fw.dve, lambda: nc.vector.tensor_tensor(t1[:], Wre, Tc, ALU.mult), reads=[W, fc], writes=[t1])
                fw.op(fw.dve, lambda: nc.vector.tensor_tensor(t2[:], Wim, Ts, ALU.mult), reads=[W, fc], writes=[t2])
                fw.op(fw.dve, lambda: nc.vector.tensor_tensor(Wr[:], t1[:], t2[:], ALU.subtract), reads=[t1, t2], writes=[Wr])
                fw.op(fw.dve, lambda: nc.vector.tensor_tensor(t1[:], Wre, Ts, ALU.mult), reads=[W, fc], writes=[t1])
                fw.op(fw.dve, lambda: nc.vector.tensor_tensor(t2[:], Wim, Tc, ALU.mult), reads=[W, fc], writes=[t2])
                fw.op(fw.dve, lambda: nc.vector.tensor_tensor(Wi[:], t1[:], t2[:], ALU.add), reads=[t1, t2], writes=[Wi])
                fw.mm(XO, XO[:], fcr, fcr[:, 640:768], Wr, Wr[:].rearrange("p c k -> p (c k)"), True, False)
                fw.mm(XO, XO[:], fcr, fcr[:, 768:896], Wi, Wi[:].rearrange("p c k -> p (c k)"), False, True)
                o_ = xo[si % 2]
                fw.op(fw.act, lambda: nc.scalar.copy(o_[:, c0:c0 + 4, :], XO[0:64, :].rearrange("p (c k) -> p c k", k=128)), reads=[XO], writes=[o_])
            if is_filter:
                fw.dma(fw.sp, KFv[0, :, ch0 * 128:(ch0 + NSB) * 128], kr[:], kr, False)
                fw.dma(fw.sp, KFv[1, :, ch0 * 128:(ch0 + NSB) * 128], ki[:], ki, False)
            else:
                o_ = xo[si % 2]
                fw.dma(fw.sp, self.CT[ch0:ch0 + NSB, 0:L].rearrange("c (a b) -> a c b", b=128)[0:64], o_[:], o_, False)
        done()

    def hy_ctxconv(self):
        nc, fw = self.nc, self.fw
        sb, ps, done = self._ctx()
        VTv = self.VT.rearrange("(c p) t -> p c t", p=128)
        KTv = self.KT.rearrange("(c p) t -> p c t", p=128)
        CTv = self.CT.rearrange("(c p) t -> p c t", p=128)
        for c in range(8):
            vv = sb([128, LC], F32, "vv")
            kf = sb([128, LC], F32, "kf")
            kb = sb([128, LC], F32, "kb")
            acc = sb([128, LC], F32, "acc")
            fw.dma(fw.sp, vv[:], VTv[:, c, L:NT], vv, True)
            fw.dma(fw.sp, kf[:], KTv[:, c, L:NT], kf, True)
            fw.dma(fw.sp, kb[:], KTv[:, 8 + c, L:NT], kb, True)
            fw.op(fw.dve, lambda: nc.vector.tensor_scalar(acc[:], vv[:], kf[:, 0:1], None, ALU.mult), reads=[vv, kf], writes=[acc])
            for l in range(1, LC):
                fw.op(fw.dve, lambda: nc.vector.scalar_tensor_tensor(acc[:, l:LC], vv[:, 0:LC - l], kf[:, l:l + 1], acc[:, l:LC], ALU.mult, ALU.add),
                      reads=[vv, kf, acc], writes=[acc])
                fw.op(fw.dve, lambda: nc.vector.scalar_tensor_tensor(acc[:, 0:LC - l], vv[:, l:LC], kb[:, l:l + 1], acc[:, 0:LC - l], ALU.mult, ALU.add),
                      reads=[vv, kb, acc], writes=[acc])
            fw.dma(fw.sp, CTv[:, c, L:NT], acc[:], acc, False)
        done()

    def hy_out(self):
        nc, fw = self.nc, self.fw
        sb, ps, done = self._ctx()
        wo = sb([128, 8, 1024], BF16, "hywo")
        stg = [sb([128, 1024], F32, "hystg") for _ in range(2)]
        self.load_w_bf(wo, self.hy_w_out.rearrange("(c p) f -> p c f", p=128), [0], stg)
        x0 = [sb([128, 8, 512], F32, "x0") for _ in range(2)]
        vv = [sb([128, 8, 512], F32, "vv") for _ in range(2)]
        cv = [sb([128, 8, 512], F32, "cv") for _ in range(2)]
        yb = sb([128, 8, 512], BF16, "yb")
        yo = [sb([128, 8, 512], F32, "yo") for _ in range(2)]
        pp = [ps([128, 512], F32, "pp") for _ in range(4)]
        VTv = self.VT.rearrange("(c p) t -> p c t", p=128)
        X0v = self.X0T.rearrange("(c p) t -> p c t", p=128)
        CTv = self.CT.rearrange("(c p) t -> p c t", p=128)

        def issue_load(ti):
            w = 512 if ti < 16 else 256
            sl = slice(ti * 512, ti * 512 + w)
            fw.dma(fw.sp, x0[ti % 2][:, :, 0:w], X0v[:, :, sl], x0[ti % 2], True)
            fw.dma(fw.sp, vv[ti % 2][:, :, 0:w], VTv[:, :, sl], vv[ti % 2], True)
            fw.dma(fw.sp, cv[ti % 2][:, :, 0:w], CTv[:, :, sl], cv[ti % 2], True)

        issue_load(0)
        n = 0
        for ti in range(17):
            w = 512 if ti < 16 else 256
            if ti + 1 < 17:
                issue_load(ti + 1)
            a, b_, c_ = x0[ti % 2], vv[ti % 2], cv[ti % 2]
            for c in range(8):
                fw.op(fw.dve, lambda: nc.vector.scalar_tensor_tensor(c_[:, c, 0:w], b_[:, c, 0:w], self.vec("hy_filt_bias", c), c_[:, c, 0:w], ALU.mult, ALU.add),
                      reads=[b_, c_, self.vecs], writes=[c_])
            fw.op(fw.dve, lambda: nc.vector.tensor_tensor(yb[:, :, 0:w], c_[:, :, 0:w], a[:, :, 0:w], ALU.mult), reads=[a, c_], writes=[yb])
            y = yo[ti % 2]
            for dc in range(8):
                p = pp[n % 4]
                n += 1
                for jc in range(8):
                    fw.mm(p, p[:, 0:w], wo, wo[:, jc, dc * 128:(dc + 1) * 128], yb, yb[:, jc, 0:w], jc == 0, jc == 7)
                fw.op(fw.act, lambda: nc.scalar.activation(y[:, dc, 0:w], p[:, 0:w], AF.Identity, bias=self.vec("hy_b_out", dc), scale=1.0),
                      reads=[p, self.vecs], writes=[y])
            fw.dma(fw.sp, self.MTv[:, :, ti * 512:ti * 512 + w], y[:, :, 0:w], y, False)
        done()


    def phase_pool(self, li, ctx_live=True):
        nc, fw = self.nc, self.fw
        st = contextlib.ExitStack()
        loc = []

        def sb(*a, **k):
            b = fw.sb(*a, stack=st, **k)
            loc.append(b)
            return b

        A = sb([128, NT], F32, "plA")
        P = [sb([128, 144, 80], F32, "plP") for _ in range(2)]
        C = [sb([128, LC + 16], F32, "plC") for _ in range(2)]
        inv = sb([128, NT], F32, "plinv")
        PLb = sb([128, NT], BF16, "plb")
        for b in P + C:
            fw.op(fw.dve, lambda: nc.vector.memset(b[:], 0.0), writes=[b])
        WINS = (2, 4, 8, 16)

        def ranges(n, k):
            lo, hi = 1, n
            rs = [(lo, hi, 1, 0)]
            sh = 1
            for lev in range(2, k + 1):
                lo, hi = lo + sh, hi - sh
                rs.append((lo, hi, sh, sh))
                sh *= 2
            return rs

        for c in range(8):
            g = c // 2
            w = WINS[g]
            k = int(math.log2(w))
            fw.dma(fw.sp, A[:], self.UTv[:, c, :], A, True)
            if c % 2 == 0:
                fw.dma(fw.sp, inv[:], self.pool_inv[g:g + 1, :].to_broadcast([128, NT]), inv, True)
            X = P[0]
            if c > 0:
                fw.op(fw.dve, lambda: nc.vector.memset(X[:, :, 0:8], 0.0), writes=[X])
                fw.op(fw.dve, lambda: nc.vector.memset(X[:, :, 72:80], 0.0), writes=[X])
                fw.op(fw.dve, lambda: nc.vector.memset(X[:, 0:8, :], 0.0), writes=[X])
                fw.op(fw.dve, lambda: nc.vector.memset(X[:, 136:144, :], 0.0), writes=[X])
            fw.op(fw.act, lambda: nc.scalar.copy(X[:, 8:136, 8:72], A[:, 0:L].rearrange("p (r q) -> p r q", q=64)),
                  reads=[A], writes=[X])
            cur = 0
            for (lo, hi, s0, s1) in ranges(80, k):
                src, dst = P[cur], P[1 - cur]
                fw.op(fw.dve, lambda: nc.vector.tensor_tensor(dst[:, :, lo:hi], src[:, :, lo - s0:hi - s0], src[:, :, lo + s1:hi + s1], ALU.add),
                      reads=[src], writes=[dst])
                cur = 1 - cur
            for (lo, hi, s0, s1) in ranges(144, k):
                src, dst = P[cur], P[1 - cur]
                fw.op(fw.dve, lambda: nc.vector.tensor_tensor(dst[:, lo:hi, 8:72], src[:, lo - s0:hi - s0, 8:72], src[:, lo + s1:hi + s1, 8:72], ALU.add),
                      reads=[src], writes=[dst])
                cur = 1 - cur
            S = P[cur]
            Sint = S[:, 8:136, 8:72]
            fw.op(fw.dve, lambda: nc.vector.tensor_tensor(Sint, Sint, inv[:, 0:L].rearrange("p (r q) -> p r q", q=64), ALU.mult),
                  reads=[S, inv], writes=[S])
            fw.op(fw.dve, lambda: nc.vector.tensor_tensor(PLb[:, 0:L].rearrange("p (r q) -> p r q", q=64), Sint,
                                                          A[:, 0:L].rearrange("p (r q) -> p r q", q=64), ALU.subtract),
                  reads=[S, A], writes=[PLb])
            X1 = C[0]
            if c > 0:
                fw.op(fw.dve, lambda: nc.vector.memset(X1[:, 0:8], 0.0), writes=[X1])
                fw.op(fw.dve, lambda: nc.vector.memset(X1[:, 8 + LC:16 + LC], 0.0), writes=[X1])
            fw.op(fw.act, lambda: nc.scalar.copy(X1[:, 8:8 + LC], A[:, L:NT]), reads=[A], writes=[X1])
            cur = 0
            for (lo, hi, s0, s1) in ranges(LC + 16, k):
                src, dst = C[cur], C[1 - cur]
                fw.op(fw.dve, lambda: nc.vector.tensor_tensor(dst[:, lo:hi], src[:, lo - s0:hi - s0], src[:, lo + s1:hi + s1], ALU.add),
                      reads=[src], writes=[dst])
                cur = 1 - cur
            S1 = C[cur]
            fw.op(fw.dve, lambda: nc.vector.tensor_tensor(S1[:, 8:8 + LC], S1[:, 8:8 + LC], inv[:, L:NT], ALU.mult),
                  reads=[S1, inv], writes=[S1])
            fw.op(fw.dve, lambda: nc.vector.tensor_tensor(PLb[:, L:NT], S1[:, 8:8 + LC], A[:, L:NT], ALU.subtract),
                  reads=[S1, A], writes=[PLb])
            fw.dma(fw.sp, self.PLTv[:, c, :], PLb[:], PLb, False)
        fw.barrier()
        fw.release(loc)
        st.close()

        st = contextlib.ExitStack()
        loc = []
        wst = sb([128, 8, 256], F32, "plws")
        wpb = sb([128, 8, 256], BF16, "plw")
        plt = [sb([128, 8, 512], BF16, "plt") for _ in range(2)]
        yo = [sb([128, 8, 512], F32, "plyo") for _ in range(2)]
        pp = [fw.ps([128, 512], F32, "plp", stack=st) for _ in range(4)]
        loc.extend(pp)
        fw.dma(fw.sp, wst[:], self.pool_w.rearrange("g (cc p) e -> p (g cc) e", p=128), wst, True)
        fw.op(fw.dve, lambda: nc.vector.tensor_copy(wpb[:], wst[:]), reads=[wst], writes=[wpb])
        ntiles = 17 if ctx_live else 16

        def issue_load(ti):
            w = 512 if ti < 16 else 256
            fw.dma(fw.sp, plt[ti % 2][:, :, 0:w], self.PLTv[:, :, ti * 512:ti * 512 + w], plt[ti % 2], True)

        issue_load(0)
        n = 0
        for ti in range(ntiles):
            w = 512 if ti < 16 else 256
            if ti + 1 < ntiles:
                issue_load(ti + 1)
            pl = plt[ti % 2]
            y = yo[ti % 2]
            for g in range(4):
                for ec in range(2):
                    p = pp[n % 4]
                    n += 1
                    for cc in range(2):
                        fw.mm(p, p[:, 0:w], wpb, wpb[:, g * 2 + cc, ec * 128:(ec + 1) * 128], pl, pl[:, 2 * g + cc, 0:w], cc == 0, cc == 1)
                    oc = 2 * g + ec
                    if n % 2 == 0:
                        fw.op(fw.dve, lambda: nc.vector.tensor_scalar(y[:, oc, 0:w], p[:, 0:w], self.vec("pool_scale", oc), None, ALU.mult),
                              reads=[p, self.vecs], writes=[y])
                    else:
                        fw.op(fw.act, lambda: nc.scalar.activation(y[:, oc, 0:w], p[:, 0:w], AF.Copy, scale=self.vec("pool_scale", oc)),
                              reads=[p, self.vecs], writes=[y])
            fw.dma(fw.sp, self.MTv[:, :, ti * 512:ti * 512 + w], y[:, :, 0:w], y, False)
        fw.barrier()
        fw.release(loc)
        st.close()

    def phase_out(self):
        nc, fw = self.nc, self.fw
        st = contextlib.ExitStack()
        loc = []

        def sb(*a, **k):
            b = fw.sb(*a, stack=st, **k)
            loc.append(b)
            return b

        hb2 = [sb([128, 8, 512], F32, "h") for _ in range(2)]
        sq = sb([128, 8, 512], BF16, "sq")
        rstd = sb([128, 512], F32, "rstd")
        hn = sb([128, 8, 512], F32, "hn")
        ob = [sb([128, 4, D], F32, "ob") for _ in range(2)]
        ps_stat = fw.ps([128, 512], F32, "pstat", stack=st)
        pt = [fw.ps([128, 8, 128], F32, "ptr", stack=st) for _ in range(2)]
        loc.extend([ps_stat] + pt)
        npt = 0
        fw.dma(fw.sp, hb2[0][:], self.HTv[:, :, 0:512], hb2[0], True)
        for ti in range(16):
            c0 = ti * 512
            hb = hb2[ti % 2]
            o = ob[ti % 2]
            if ti + 1 < 16:
                fw.dma(fw.sp, hb2[(ti + 1) % 2][:], self.HTv[:, :, c0 + 512:c0 + 1024], hb2[(ti + 1) % 2], True)
            self.rms_stats(hb, 512, sq, ps_stat, rstd)
            for c in range(8):
                fw.op(fw.dve, lambda: nc.vector.scalar_tensor_tensor(hn[:, c, :], hb[:, c, :], self.vec("final_gain", c),
                                                                     rstd[:], ALU.mult, ALU.mult),
                      reads=[hb, rstd, self.vecs], writes=[hn])
            for g in range(4):
                p = pt[npt % 2]
                npt += 1
                for c in range(8):
                    fw.op(fw.pe, lambda: nc.tensor.transpose(p[:, c, :], hn[:, c, g * 128:(g + 1) * 128], self.ident[:]),
                          reads=[hn, self.ident], writes=[p])
                if g % 2 == 0:
                    fw.op(fw.dve, lambda: nc.vector.tensor_copy(o[:, g, :], p[:].rearrange("p c t -> p (c t)")), reads=[p], writes=[o])
                else:
                    fw.op(fw.act, lambda: nc.scalar.copy(o[:, g, :], p[:].rearrange("p c t -> p (c t)")), reads=[p], writes=[o])
            fw.dma(fw.sp, self.outT[c0:c0 + 512, :].rearrange("(g p) d -> p g d", p=128), o[:], o, False)
        fw.barrier()
        fw.release(loc)
        st.close()


def _pc(v):
    v = np.asarray(v, np.float32).reshape(-1, 128)
    return np.ascontiguousarray(v.T)


def _window_bounds(n, w):
    pos = np.arange(n)
    return np.clip(pos - w // 2, 0, n), np.clip(pos + (w - w // 2), 0, n)


def host_consts():
    c = {}
    c["ident"] = np.eye(128, dtype=np.float32)
    inv = np.zeros((4, NT), np.float32)
    for gi, w in enumerate((2, 4, 8, 16)):
        rlo, rhi = _window_bounds(128, w)
        clo, chi = _window_bounds(64, w)
        cnt = ((rhi - rlo)[:, None] * (chi - clo)[None, :]).astype(np.float32)
        inv[gi, :L] = (1.0 / cnt).reshape(-1)
        lo, hi = _window_bounds(LC, w)
        inv[gi, L:] = 1.0 / (hi - lo).astype(np.float32)
    c["pool_inv"] = inv
    tri = np.zeros((4, 128, 128), np.float32)
    blk = np.arange(128) // 32
    same = blk[:, None] == blk[None, :]
    sidx = np.arange(128)[:, None]
    tidx = np.arange(128)[None, :]
    tri[0] = (same & (sidx <= tidx)).astype(np.float32)
    tri[1] = (same & (sidx >= tidx)).astype(np.float32)
    tri[2] = same.astype(np.float32)
    for c_ in range(4):
        tri[3, :, c_] = (blk == c_).astype(np.float32)
    c["tri"] = tri
    def feats(Lx):
        pos = np.arange(Lx, dtype=np.float32)
        t = np.linspace(0.0, 1.0, Lx, dtype=np.float32)
        bands = np.linspace(1e-4, 15, 16, dtype=np.float32)
        ang = (np.float32(2 * math.pi) * pos / np.float32(Lx))[:, None] * bands[None, :]
        return np.concatenate([t[:, None], np.cos(ang), -np.sin(ang)], axis=1).astype(np.float32), t
    fl, tl = feats(L)
    fcx, tcx = feats(LC)
    hf_ = np.zeros((128, NT), np.float32)
    hf_[:33] = np.concatenate([fl, fcx], axis=0).T
    c["hy_feats"] = hf_
    c["hy_tpos"] = np.concatenate([tl, tcx])[None, :].astype(np.float32)
    deltas = np.abs(np.linspace(math.log(1e-2) / 1.5, math.log(1e-2) / 0.3, D, dtype=np.float32))
    c["negdelta"] = (-deltas).astype(np.float32)
    a = np.arange(128, dtype=np.float64)
    ang = 2 * np.pi * np.outer(a, a) / 128.0
    C, S = np.cos(ang), np.sin(ang)
    angN = 2 * np.pi * np.outer(a, a) / 16384.0
    Tc, Ts = np.cos(angN), np.sin(angN)
    fftc = np.concatenate([-S, C, S, C, -S, C / 16384.0, -S / 16384.0, np.tile(Tc, (1, 4)), np.tile(Ts, (1, 4))], axis=1)
    c["fftc"] = fftc.astype(np.float32)
    return c


def make_in_maps(inputs, ncores=NCORES):
    f = lambda k: np.ascontiguousarray(np.asarray(inputs[k], np.float32))
    consts = host_consts()
    vecs = np.zeros((128, NV), np.float32)

    def put(name, v):
        off, n = VEC_LAY[name]
        vecs[:, off:off + n] = _pc(v)

    put("final_gain", f("final_gain"))
    for j in range(2):
        put("hg_gain%d" % j, f("hg_norm_gain")[j])
    put("pool_scale", f("pool_scale")[0])
    put("hy_b_in", f("hy_b_in")[0])
    for t in range(3):
        put("hy_conv_w%d" % t, f("hy_conv_w")[0, t])
    put("hy_conv_b", f("hy_conv_b")[0])
    put("hy_filt_bias", f("hy_filt_bias")[0])
    put("hy_b_out", f("hy_b_out")[0])
    put("hy_negdelta", consts["negdelta"])
    shared = {
        "w_ada": f("w_ada"), "b_ada": f("b_ada"),
        "ffn_w_gate": f("ffn_w_gate"), "ffn_w_up": f("ffn_w_up"), "ffn_w_down": f("ffn_w_down"),
        "vecs": vecs, "ident": consts["ident"], "pool_w": f("pool_w")[0], "pool_inv": consts["pool_inv"],
        "hg_w_in": f("hg_w_in"), "hg_w_out": f("hg_w_out"), "hg_lb_logits": f("hg_lb_logits"), "hg_norm_gain": f("hg_norm_gain"),
        "tri": consts["tri"],
        "hy_w_in": f("hy_w_in")[0], "hy_w_out": f("hy_w_out")[0], "hy_w1": f("hy_w1")[0], "hy_w2": f("hy_w2")[0],
        "hy_w3": f("hy_w3")[0], "hy_w4": f("hy_w4")[0],
        "hy_sfb": np.ascontiguousarray(np.stack([f("hy_sin_freq")[0], f("hy_b1")[0], f("hy_b2")[0], f("hy_b3")[0]], axis=1)),
        "hy_feats": consts["hy_feats"], "hy_tpos": consts["hy_tpos"], "fftc": consts["fftc"],
    }
    maps = []
    x = f("x")
    ctx = f("ctx")
    c = f("c")
    cc = f("c_ctx")
    for i in range(ncores):
        b = i % 4
        m = dict(shared)
        m["x"] = x[b]
        m["ctx"] = ctx[b]
        cs = np.stack([c[b], cc], axis=-1)
        m["csT"] = np.ascontiguousarray(cs.reshape(8, 128, 2).transpose(1, 0, 2))
        maps.append(m)
    return maps


_PROG_CACHE = {}


def kernel(**inputs):
    if "main" not in _PROG_CACHE:
        _PROG_CACHE["main"] = Prog().build()
    nc = _PROG_CACHE["main"]
    maps = make_in_maps(inputs)
    res = run_bass_kernel_spmd(nc, maps, core_ids=list(range(NCORES)))
    out = np.stack([np.asarray(res.results[b]["out"], np.float32) for b in range(4)], axis=0)
    return out
```
